# Optimizing a Trainium2 kernel written in Bass

```python
import math, functools
import jax, jax.numpy as jnp
from jax import lax
import numpy as np

D_MODEL = 1024
BATCH = 4
SEQ = 4096
DEPTH = 2

A_HEADS = 4
A_QK_DIM = 32
A_V_DIM = 2 * A_QK_DIM
B_HEADS = 4
B_HEAD_DIM = 64
C_HEADS = 4
C_HEAD_DIM = 128
CONV_K = 4
CHUNK = 64
Q_BLOCK = 128
D_FF = 2816
A_WIDTH = A_HEADS * A_V_DIM
B_WIDTH = B_HEADS * B_HEAD_DIM
C_WIDTH = C_HEADS * C_HEAD_DIM
MIX_WIDTH = A_WIDTH + B_WIDTH + C_WIDTH
SPLIT_SIZES = (A_HEADS * 2 * A_QK_DIM, A_HEADS * 2 * A_QK_DIM, A_WIDTH, B_WIDTH, B_WIDTH, B_WIDTH, 3 * C_WIDTH, C_WIDTH, C_HEADS, C_HEADS)
N_IN = sum(SPLIT_SIZES)
DEEPNORM_ALPHA = (2.0 * DEPTH) ** 0.25
DEEPNORM_BETA = (8.0 * DEPTH) ** -0.25
LN_EPS = 1e-5
RMS_EPS = 1e-6

kernel_name = 'hybrid_diff_stick_deltanet_macaron'


def layer_norm(x, g, b):
    xf = x.astype(jnp.float32)
    mu = jnp.mean(xf, axis=-1, keepdims=True)
    var = jnp.mean(jnp.square(xf - mu), axis=-1, keepdims=True)
    return ((xf - mu) * lax.rsqrt(var + LN_EPS) * g.astype(jnp.float32) + b.astype(jnp.float32)).astype(x.dtype)


def rms_norm(x, g):
    xf = x.astype(jnp.float32)
    y = xf * lax.rsqrt(jnp.mean(jnp.square(xf), axis=-1, keepdims=True) + RMS_EPS)
    return (y * g.astype(jnp.float32)).astype(x.dtype)


def l2_normalize(x):
    xf = x.astype(jnp.float32)
    return xf * lax.rsqrt(jnp.sum(jnp.square(xf), axis=-1, keepdims=True) + RMS_EPS)


def swiglu(x, w_gu, w_down):
    gate, up = jnp.split(x @ w_gu, 2, axis=-1)
    return (jax.nn.silu(gate) * up) @ w_down


def sweep_query_blocks(block_fn, q):
    b, h, s, d = q.shape
    nb = s // Q_BLOCK
    q_blocks = jnp.moveaxis(q.reshape(b, h, nb, Q_BLOCK, d), 2, 0)
    starts = jnp.arange(nb, dtype=jnp.int32) * Q_BLOCK
    out = lax.map(lambda qs: block_fn(qs[0], qs[1]), (q_blocks, starts))
    return jnp.moveaxis(out, 0, 2).reshape(b, h, s, out.shape[-1])


def diff_attention_block(q_blk, start, k1, k2, v, lam):
    q1, q2 = jnp.split(q_blk, 2, axis=-1)
    t = start + jnp.arange(Q_BLOCK)
    s = jnp.arange(k1.shape[2])
    causal = s[None, :] <= t[:, None]
    scale = A_QK_DIM ** -0.5

    def probs(qi, ki):
        sc = jnp.einsum('bhqd,bhkd->bhqk', qi, ki).astype(jnp.float32) * scale
        return jax.nn.softmax(jnp.where(causal, sc, -jnp.inf), axis=-1)

    w = probs(q1, k1) - lam * probs(q2, k2)
    return jnp.einsum('bhqk,bhkd->bhqd', w.astype(v.dtype), v)


def stick_breaking_block(q_blk, start, k, v):
    t = start + jnp.arange(Q_BLOCK)
    s = jnp.arange(k.shape[2])
    strict = s[None, :] < t[:, None]
    z = jnp.einsum('bhqd,bhkd->bhqk', q_blk, k).astype(jnp.float32) * (B_HEAD_DIM ** -0.5)
    log_stay = jnp.where(strict, jax.nn.log_sigmoid(-z), 0.0)
    log_tail = lax.cumsum(log_stay, axis=3, reverse=True) - log_stay
    w = jnp.where(strict, jnp.exp(jax.nn.log_sigmoid(z) + log_tail), 0.0)
    return jnp.einsum('bhqk,bhkd->bhqd', w.astype(v.dtype), v)


def causal_depthwise_conv(x, w):
    return lax.conv_general_dilated(x, w[:, None, :].astype(x.dtype), window_strides=(1,), padding=[(w.shape[0] - 1, 0)], dimension_numbers=('NWC', 'WIO', 'NWC'), feature_group_count=x.shape[-1])


def gated_delta_rule(q, k, v, g, beta):
    b, h, s, dk = q.shape
    dv = v.shape[-1]
    n = s // CHUNK
    q = q.astype(jnp.float32) * (dk ** -0.5)
    k = k.astype(jnp.float32)
    v = v.astype(jnp.float32)
    chunks = lambda a: a.reshape(b, h, n, CHUNK, *a.shape[3:])
    q, k, v, g, beta = chunks(q), chunks(k), chunks(v), chunks(g), chunks(beta)
    g = jnp.cumsum(g, axis=-1)
    lower = jnp.tril(jnp.ones((CHUNK, CHUNK), dtype=bool))
    strict = jnp.tril(jnp.ones((CHUNK, CHUNK), dtype=bool), -1)
    gdiff = g[..., :, None] - g[..., None, :]
    decay = jnp.where(lower, jnp.exp(jnp.where(lower, gdiff, 0.0)), 0.0)
    k_beta = k * beta[..., None]
    l_mat = jnp.where(strict, jnp.einsum('bhncd,bhned->bhnce', k_beta, k) * decay, 0.0)
    t_mat = l_mat + jnp.eye(CHUNK, dtype=jnp.float32)
    rhs = jnp.concatenate([v * beta[..., None], k_beta * jnp.exp(g)[..., None]], axis=-1)
    sol = lax.linalg.triangular_solve(t_mat, rhs, left_side=True, lower=True, unit_diagonal=True)
    u, w = sol[..., :dv], sol[..., dv:]
    intra = jnp.where(lower, jnp.einsum('bhncd,bhned->bhnce', q, k) * decay, 0.0)

    def step(state, inp):
        q_i, k_i, u_i, w_i, g_i, a_i = inp
        v_new = u_i - jnp.einsum('bhck,bhkv->bhcv', w_i, state)
        out = jnp.einsum('bhck,bhkv->bhcv', q_i * jnp.exp(g_i)[..., None], state) + jnp.einsum('bhcs,bhsv->bhcv', a_i, v_new)
        g_last = g_i[..., -1:]
        state = state * jnp.exp(g_last)[..., None] + jnp.einsum('bhck,bhcv->bhkv', k_i * jnp.exp(g_last - g_i)[..., None], v_new)
        return state, out

    xs = tuple(jnp.moveaxis(a, 2, 0) for a in (q, k, u, w, g, intra))
    state0 = jnp.zeros((b, h, dk, dv), jnp.float32)
    _, out = lax.scan(step, state0, xs)
    return jnp.moveaxis(out, 0, 2).reshape(b, h, s, dv)


def hybrid_mixer(xn, w_in, conv_w, dn_a_log, dn_dt_bias, dn_norm_g, diff_lambda, diff_norm_g, sb_norm_g, w_out, lambda_init):
    b, s, _ = xn.shape
    points = np.cumsum(SPLIT_SIZES)[:-1].tolist()
    qa, ka, va, qb, kb, vb, qkv_c, z_c, beta_c, a_c = jnp.split(xn @ w_in, points, axis=-1)
    heads = lambda t, nh: t.reshape(b, s, nh, -1).transpose(0, 2, 1, 3)
    tokens = lambda t: t.transpose(0, 2, 1, 3).reshape(b, s, -1)

    k1, k2 = jnp.split(heads(ka, A_HEADS), 2, axis=-1)
    lf = diff_lambda.astype(jnp.float32)
    lam = jnp.exp(jnp.sum(lf[0] * lf[1])) - jnp.exp(jnp.sum(lf[2] * lf[3])) + lambda_init
    oa = sweep_query_blocks(functools.partial(diff_attention_block, k1=k1, k2=k2, v=heads(va, A_HEADS), lam=lam), heads(qa, A_HEADS))
    oa = rms_norm(oa, diff_norm_g) * (1.0 - lambda_init)

    ob = sweep_query_blocks(functools.partial(stick_breaking_block, k=heads(kb, B_HEADS), v=heads(vb, B_HEADS)), heads(qb, B_HEADS))
    ob = rms_norm(ob, sb_norm_g)

    qc, kc, vc = jnp.split(jax.nn.silu(causal_depthwise_conv(qkv_c, conv_w)), 3, axis=-1)
    g = -jnp.exp(dn_a_log.astype(jnp.float32)) * jax.nn.softplus(a_c.astype(jnp.float32) + dn_dt_bias.astype(jnp.float32))
    beta = jax.nn.sigmoid(beta_c.astype(jnp.float32))
    oc = gated_delta_rule(l2_normalize(heads(qc, C_HEADS)), l2_normalize(heads(kc, C_HEADS)), heads(vc, C_HEADS), g.transpose(0, 2, 1), beta.transpose(0, 2, 1))
    oc = rms_norm(oc, dn_norm_g) * jax.nn.silu(heads(z_c, C_HEADS).astype(jnp.float32))

    o = jnp.concatenate([tokens(oa).astype(xn.dtype), tokens(ob).astype(xn.dtype), tokens(oc).astype(xn.dtype)], axis=-1)
    return o @ w_out


def setup_inputs(seed: int = 0) -> dict:
    key = jax.random.key(seed)
    ks = jax.random.split(key, 16)
    f32 = jnp.float32
    x = jax.random.normal(ks[0], (BATCH, SEQ, D_MODEL), f32)
    ffn1_w_gu = jax.random.normal(ks[1], (DEPTH, D_MODEL, 2 * D_FF), f32) * D_MODEL ** -0.5
    ffn1_w_down = jax.random.normal(ks[2], (DEPTH, D_FF, D_MODEL), f32) * (D_FF ** -0.5 * DEEPNORM_BETA)
    ffn2_w_gu = jax.random.normal(ks[3], (DEPTH, D_MODEL, 2 * D_FF), f32) * D_MODEL ** -0.5
    ffn2_w_down = jax.random.normal(ks[4], (DEPTH, D_FF, D_MODEL), f32) * (D_FF ** -0.5 * DEEPNORM_BETA)
    ln_g = 1.0 + 0.02 * jax.random.normal(ks[5], (DEPTH, 3, D_MODEL), f32)
    ln_b = 0.02 * jax.random.normal(ks[6], (DEPTH, 3, D_MODEL), f32)
    w_in = jax.random.normal(ks[7], (DEPTH, D_MODEL, N_IN), f32) * D_MODEL ** -0.5
    conv_w = jax.random.normal(ks[8], (DEPTH, CONV_K, 3 * C_WIDTH), f32) * CONV_K ** -0.5
    dn_a_log = jnp.log(jax.random.uniform(ks[9], (DEPTH, C_HEADS), f32, 1.0, 16.0))
    dt = jnp.exp(jax.random.uniform(ks[10], (DEPTH, C_HEADS), f32, math.log(1e-3), math.log(1e-1)))
    dn_dt_bias = dt + jnp.log(-jnp.expm1(-dt))
    dn_norm_g = 1.0 + 0.02 * jax.random.normal(ks[11], (DEPTH, C_HEAD_DIM), f32)
    diff_lambda = 0.1 * jax.random.normal(ks[12], (DEPTH, 4, A_QK_DIM), f32)
    diff_norm_g = 1.0 + 0.02 * jax.random.normal(ks[13], (DEPTH, A_V_DIM), f32)
    sb_norm_g = 1.0 + 0.02 * jax.random.normal(ks[14], (DEPTH, B_HEAD_DIM), f32)
    w_out = jax.random.normal(ks[15], (DEPTH, MIX_WIDTH, D_MODEL), f32) * (MIX_WIDTH ** -0.5 * DEEPNORM_BETA)
    return {'x': x, 'ffn1_w_gu': ffn1_w_gu, 'ffn1_w_down': ffn1_w_down, 'ffn2_w_gu': ffn2_w_gu, 'ffn2_w_down': ffn2_w_down, 'ln_g': ln_g, 'ln_b': ln_b, 'w_in': w_in, 'conv_w': conv_w, 'dn_a_log': dn_a_log, 'dn_dt_bias': dn_dt_bias, 'dn_norm_g': dn_norm_g, 'diff_lambda': diff_lambda, 'diff_norm_g': diff_norm_g, 'sb_norm_g': sb_norm_g, 'w_out': w_out}


def reference(x, ffn1_w_gu, ffn1_w_down, ffn2_w_gu, ffn2_w_down, ln_g, ln_b, w_in, conv_w, dn_a_log, dn_dt_bias, dn_norm_g, diff_lambda, diff_norm_g, sb_norm_g, w_out):
    for l in range(DEPTH):
        lambda_init = 0.8 - 0.6 * math.exp(-0.3 * l)
        x = layer_norm(DEEPNORM_ALPHA * x + 0.5 * swiglu(x, ffn1_w_gu[l], ffn1_w_down[l]), ln_g[l, 0], ln_b[l, 0])
        mix = hybrid_mixer(x, w_in[l], conv_w[l], dn_a_log[l], dn_dt_bias[l], dn_norm_g[l], diff_lambda[l], diff_norm_g[l], sb_norm_g[l], w_out[l], lambda_init)
        x = layer_norm(DEEPNORM_ALPHA * x + mix, ln_g[l, 1], ln_b[l, 1])
        x = layer_norm(DEEPNORM_ALPHA * x + 0.5 * swiglu(x, ffn2_w_gu[l], ffn2_w_down[l]), ln_g[l, 2], ln_b[l, 2])
    return x
```

```python
from contextlib import ExitStack
import math
import numpy as np
import concourse.bass as bass
import concourse.mybir as mybir
from concourse.bass_utils import run_bass_kernel_spmd

F32 = mybir.dt.float32
BF16 = mybir.dt.bfloat16
AF = mybir.ActivationFunctionType
ALU = mybir.AluOpType

D = 1024
DFF = 2816
NFC = DFF // 128
DEPTH = 2
SEQ = 4096
NIN = 3592
ALPHA = (2.0 * DEPTH) ** 0.25
LN_EPS = 1e-5
RMS_EPS = 1e-6

ENGS = ("pe", "act", "dve", "pool", "sp")
RG = [[0, 1], [2, 3], [4, 5], [6, 7]]
NH = 2
BLK = {"pe": "tensor", "act": "scalar", "dve": "vector", "pool": "gpsimd", "sp": "sync"}
SAME_ENG_SYNC = True
_uid = [0]
_tagc = [0]


def _tag():
    _tagc[0] += 1
    return f"_p{_tagc[0]}"


class Op:
    __slots__ = ("eng", "fn", "deps", "signals", "sigval", "isdma", "dsem", "dval", "dinc")


class Prog:
    def __init__(self, nc):
        self.nc = nc
        self.q = {e: [] for e in ENGS}
        self.lastw = {}
        self.readers = {}
        self.dtot = {}

    def op(self, eng, fn, reads=(), writes=(), dsem=None, dinc=16):
        o = Op()
        o.eng, o.fn, o.signals, o.sigval = eng, fn, False, 0
        o.isdma = dsem is not None
        o.dsem = dsem
        o.dval = 0
        o.dinc = dinc
        if o.isdma:
            self.dtot[dsem] = self.dtot.get(dsem, 0) + dinc
            o.dval = self.dtot[dsem]
        deps = []
        for k in reads:
            p = self.lastw.get(k)
            if p is not None:
                deps.append((p, "raw"))
        for k in writes:
            p = self.lastw.get(k)
            if p is not None:
                deps.append((p, "waw"))
            for p in self.readers.get(k, {}).values():
                deps.append((p, "war"))
        o.deps = []
        for p, kind in deps:
            if p is o:
                continue
            if not p.isdma and not o.isdma and p.eng == eng:
                if eng == "pe" or not SAME_ENG_SYNC:
                    continue
            if not p.isdma:
                p.signals = True
            o.deps.append(p)
        for k in reads:
            sk = ("d", dsem) if o.isdma else eng
            self.readers.setdefault(k, {})[sk] = o
        for k in writes:
            self.lastw[k] = o
            self.readers[k] = {}
        self.q[eng].append(o)
        return o

    def emit(self):
        nc = self.nc
        _uid[0] += 1
        u = _uid[0]
        esem = {}
        for e in ENGS:
            c = 0
            for o in self.q[e]:
                if o.signals and not o.isdma:
                    c += 1
                    o.sigval = c
            if c:
                esem[e] = nc.alloc_semaphore(name=f"s{u}_{e}")
        dsems = {}
        for i, k in enumerate(self.dtot):
            dsems[k] = nc.alloc_semaphore(name=f"d{u}_{i}")
        with nc.Block() as block:
            for e in ENGS:
                if not self.q[e] and e != "sp":
                    continue

                def body(eng, e=e):
                    seen = {}
                    for o in self.q[e]:
                        need = {}
                        for p in o.deps:
                            if p.isdma:
                                sk, val = ("d", p.dsem), p.dval
                            else:
                                sk, val = p.eng, p.sigval
                            if val > need.get(sk, 0):
                                need[sk] = val
                        for sk, val in need.items():
                            if seen.get(sk, 0) >= val:
                                continue
                            seen[sk] = val
                            h = dsems[sk[1]] if isinstance(sk, tuple) else esem[sk]
                            eng.wait_ge(h, val)
                        ins = o.fn(eng)
                        if o.isdma:
                            ins.then_inc(dsems[o.dsem], o.dinc)
                        elif o.signals:
                            ins.then_inc(esem[e], 1)
                    if e == "sp":
                        for k, tot in self.dtot.items():
                            eng.wait_ge(dsems[k], tot)

                getattr(block, BLK[e])(body)
        nc.clear_and_free_semaphores(list(esem.values()) + list(dsems.values()))
        nc.all_engine_barrier()


class Rot:
    def __init__(self, n):
        self.n, self.i = n, -1

    def next(self):
        self.i = (self.i + 1) % self.n
        return self.i


def load_cast_weight(P, pool, name, dram_ap, nk, ncols, stage, stage_rot, dst, chunk):
    for k in range(nk):
        for c0 in range(0, ncols, chunk):
            cw = min(chunk, ncols - c0)
            s = stage_rot.next()
            P.op("sp", lambda e, s=s, k=k, c0=c0, cw=cw: e.dma_start(
                out=stage[:, s, 0:cw], in_=dram_ap[k * 128:(k + 1) * 128, c0:c0 + cw]),
                writes=[("stg", s)], dsem=("stg", s))
            P.op(pool, lambda e, s=s, k=k, c0=c0, cw=cw: e.tensor_copy(
                out=dst[:, k, c0:c0 + cw], in_=stage[:, s, 0:cw]),
                reads=[("stg", s)], writes=[(name, k)])


def transpose_block(P, ident, src, src_key, nt, psT, psT_rot, dst, dst_key, evac_engs, nkc=8,
                    scale=None):
    for kc in range(nkc):
        b = psT_rot.next()
        for t in range(nt):
            P.op("pe", lambda e, b=b, t=t, kc=kc: e.transpose(
                out=psT[:, b, t * 128:(t + 1) * 128], in_=src[:, t, kc * 128:(kc + 1) * 128],
                identity=ident[:]),
                reads=[src_key], writes=[("psT", b)])
        ev = evac_engs[kc % len(evac_engs)]
        if ev == "act":
            P.op("act", lambda e, b=b, kc=kc: e.activation(
                out=dst[:, kc, 0:nt * 128], in_=psT[:, b, 0:nt * 128], func=AF.Copy,
                scale=(1.0 if scale is None else scale)),
                reads=[("psT", b)], writes=[(dst_key, kc)])
        else:
            P.op(ev, lambda e, b=b, kc=kc: e.tensor_copy(
                out=dst[:, kc, 0:nt * 128], in_=psT[:, b, 0:nt * 128]),
                reads=[("psT", b)], writes=[(dst_key, kc)])


def ln_tile(P, z, zkey, s, t, gam, bet, eps, st6, mv, rs, cm05, gk="gam", bk="bet"):
    ln_stats(P, z, zkey, s, t, st6, mv)
    P.op("dve", lambda e: e.tensor_scalar(rs[:, t:t + 1], mv[:, t, 1:2], eps, None, ALU.add),
         reads=[("mv", t)], writes=[("rs", t)])
    P.op("pool", lambda e: e.tensor_tensor(out=rs[:, t:t + 1], in0=rs[:, t:t + 1], in1=cm05[:], op=ALU.pow),
         reads=[("rs", t), "cm05"], writes=[("rs", t)])
    P.op("dve", lambda e: e.scalar_tensor_tensor(out=z[:, s, :], in0=z[:, s, :], scalar=mv[:, t, 0:1], in1=gam[:],
                                                 op0=ALU.subtract, op1=ALU.mult),
         reads=[zkey, ("mv", t), gk], writes=[zkey])
    P.op("dve", lambda e: e.scalar_tensor_tensor(out=z[:, s, :], in0=z[:, s, :], scalar=rs[:, t:t + 1], in1=bet[:],
                                                 op0=ALU.mult, op1=ALU.add),
         reads=[zkey, ("rs", t), bk], writes=[zkey])


def ln_stats(P, z, zkey, s, t, st6, mv):
    for h in range(2):
        P.op("dve", lambda e, h=h: e.bn_stats(out=st6[:, t, h, :], in_=z[:, s, h * 512:(h + 1) * 512]),
             reads=[zkey], writes=[("st6", t, h)])
    P.op("dve", lambda e: e.bn_aggr(out=mv[:, t, 0:2], in_=st6[:, t, :, :]),
         reads=[("st6", t, 0), ("st6", t, 1)], writes=[("mv", t)])


def ln_rstd(P, mv, rs, nt, eps):
    P.op("act", lambda e: e.activation(out=rs[:, 0:nt], in_=mv[:, 0:nt, 1], func=AF.Sqrt, bias=eps, scale=1.0),
         reads=[("mv", t) for t in range(nt)], writes=["rs0"])
    P.op("dve", lambda e: e.reciprocal(out=rs[:, 0:nt], in_=rs[:, 0:nt]), reads=["rs0"], writes=["rs"])


def ln_apply(P, z, zkey, s, t, gam, bet, mv, rs, gk="gam", bk="bet"):
    P.op("dve", lambda e: e.tensor_scalar(z[:, s, :], z[:, s, :], mv[:, t, 0:1], rs[:, t:t + 1],
                                          ALU.subtract, ALU.mult),
         reads=[zkey, ("mv", t), "rs"], writes=[zkey])
    P.op("pool", lambda e: e.tensor_tensor(out=z[:, s, :], in0=z[:, s, :], in1=gam[:], op=ALU.mult),
         reads=[zkey, gk], writes=[zkey])
    P.op("pool", lambda e: e.tensor_tensor(out=z[:, s, :], in0=z[:, s, :], in1=bet[:], op=ALU.add),
         reads=[zkey, bk], writes=[zkey])


def ffn_phase(nc, T, x_in, x_out, w_gu, w_down, g_ff, b_ff, ident_d, ag=None):
    NB = T // 512
    NS = 5
    with ExitStack() as es:
        tg = _tag()
        sb = lambda n, s, d: es.enter_context(nc.sbuf_tensor(n + tg, s, d))
        ps = lambda n, s, d: es.enter_context(nc.psum_tensor(n + tg, s, d))
        wgu = sb("wgu", [128, 8, 2 * DFF], BF16)
        wd = sb("wd", [128, NFC, D], BF16)
        stage = sb("stage", [128, 4, 512], F32)
        xs = sb("xs", [128, NS, D], F32)
        xT = sb("xT", [128, 8, 512], BF16)
        aT = sb("aT", [128, NFC, 512], BF16)
        sg = sb("sg", [128, 2, 512], F32)
        gam = sb("gam", [128, D], F32)
        bet = sb("bet", [128, D], F32)
        ident = sb("ident", [128, 128], F32)
        st6 = sb("st6", [128, 4, 2, 6], F32)
        mv = sb("mv", [128, 4, 2], F32)
        rs = sb("rs", [128, 4], F32)
        xb = sb("xb", [128, D], BF16)
        cm05 = sb("cm05", [128, 1], F32)
        psT = ps("psT", [128, 2, 512], F32)
        psG = ps("psG", [128, 2, 512], F32)
        psU = ps("psU", [128, 2, 512], F32)
        psY = ps("psY", [128, 2, 512], F32)

        P = Prog(nc)
        P.op("sp", lambda e: e.dma_start(out=ident[:], in_=ident_d[:, :]), writes=["ident"], dsem="c0")
        P.op("sp", lambda e: e.dma_start(out=gam[:], in_=g_ff.partition_broadcast(128)), writes=["gam"], dsem="c1")
        P.op("sp", lambda e: e.dma_start(out=bet[:], in_=b_ff.partition_broadcast(128)), writes=["bet"], dsem="c2")

        NT = T // 128

        def load_tile(g):
            if g >= NT:
                return
            s = g % NS
            P.op("sp", lambda e: e.dma_start(out=xs[:, s, :], in_=x_in[g * 128:(g + 1) * 128, :]),
                 writes=[("xs", s)], dsem=("xl", s))

        for g in range(NS):
            load_tile(g)
        srot = Rot(4)
        ceng = ["pool", "dve", "act", "dve"]
        cnt = [0]

        def wload(dst, dkey, src, k, c0, cw):
            sidx = srot.next()
            P.op("sp", lambda e: e.dma_start(out=stage[:, sidx, 0:cw], in_=src[k * 128:(k + 1) * 128, c0:c0 + cw]),
                 writes=[("stg", sidx)], dsem=("stg", sidx))
            en = ceng[cnt[0] % len(ceng)]
            cnt[0] += 1
            if en == "act":
                P.op("act", lambda e: e.activation(out=dst[:, k, c0:c0 + cw], in_=stage[:, sidx, 0:cw], func=AF.Copy),
                     reads=[("stg", sidx)], writes=[dkey])
            else:
                P.op(en, lambda e: e.tensor_copy(out=dst[:, k, c0:c0 + cw], in_=stage[:, sidx, 0:cw]),
                     reads=[("stg", sidx)], writes=[dkey])

        def load_gu_group(cgi):
            cg = cgi * 512
            cw_ = min(512, DFF - cg)
            for half in range(2):
                for kc in range(8):
                    wload(wgu, ("wgu", kc, half, cgi), w_gu, kc, half * DFF + cg, cw_)

        def load_wd():
            for fc in range(NFC):
                for h in range(2):
                    wload(wd, ("wd", fc, h), w_down, fc, h * 512, 512)

        P.op("pool", lambda e: e.memset(cm05[:], -0.5), writes=["cm05"])
        rT, rG, rY = Rot(2), Rot(2), Rot(2)
        for blk in range(NB):
            sl = [(blk * 4 + t) % NS for t in range(4)]
            for kc in range(8):
                b = rT.next()
                for t in range(4):
                    P.op("pe", lambda e, b=b, t=t, kc=kc, s=sl[t]: e.transpose(
                        out=psT[:, b, t * 128:(t + 1) * 128], in_=xs[:, s, kc * 128:(kc + 1) * 128],
                        identity=ident[:]),
                        reads=[("xs", sl[t]), "ident"], writes=[("psT", b)])
                P.op("act", lambda e, b=b, kc=kc: e.activation(out=xT[:, kc, :], in_=psT[:, b, :], func=AF.Copy),
                     reads=[("psT", b)], writes=[("xT", kc)])
            for fc in range(NFC):
                if blk == 0 and fc % 4 == 0:
                    load_gu_group(fc // 4)
                b = rG.next()
                for kc in range(8):
                    P.op("pe", lambda e, b=b, kc=kc, fc=fc: e.matmul(
                        psG[:, b, :], lhsT=wgu[:, kc, fc * 128:(fc + 1) * 128], rhs=xT[:, kc, :],
                        start=(kc == 0), stop=(kc == 7)),
                        reads=[("wgu", kc, 0, fc // 4), ("xT", kc)], writes=[("psG", b)])
                for kc in range(8):
                    P.op("pe", lambda e, b=b, kc=kc, fc=fc: e.matmul(
                        psU[:, b, :], lhsT=wgu[:, kc, DFF + fc * 128:DFF + (fc + 1) * 128], rhs=xT[:, kc, :],
                        start=(kc == 0), stop=(kc == 7)),
                        reads=[("wgu", kc, 1, fc // 4), ("xT", kc)], writes=[("psU", b)])
                P.op("act", lambda e, b=b: e.activation(out=sg[:, b, :], in_=psG[:, b, :], func=AF.Silu),
                     reads=[("psG", b)], writes=[("sg", b)])
                P.op("dve", lambda e, b=b, fc=fc: e.tensor_tensor(
                    out=aT[:, fc, :], in0=psU[:, b, :], in1=sg[:, b, :], op=ALU.mult),
                    reads=[("psU", b), ("sg", b)], writes=[("aT", fc)])
            if blk == 0:
                load_wd()
            for t in range(4):
                s = sl[t]
                for h in range(2):
                    b = rY.next()
                    for fc in range(NFC):
                        P.op("pe", lambda e, b=b, fc=fc, t=t, h=h: e.matmul(
                            psY[:, b, :], lhsT=aT[:, fc, t * 128:(t + 1) * 128], rhs=wd[:, fc, h * 512:(h + 1) * 512],
                            start=(fc == 0), stop=(fc == NFC - 1)),
                            reads=[("aT", fc), ("wd", fc, h)], writes=[("psY", b)])
                    P.op("dve", lambda e, b=b, h=h, s=s: e.scalar_tensor_tensor(
                        out=xs[:, s, h * 512:(h + 1) * 512], in0=psY[:, b, :], scalar=0.5 / ALPHA,
                        in1=xs[:, s, h * 512:(h + 1) * 512], op0=ALU.mult, op1=ALU.add),
                        reads=[("psY", b), ("xs", s)], writes=[("xs", s)])
                ln_tile(P, xs, ("xs", s), s, t, gam, bet, LN_EPS / ALPHA ** 2, st6, mv, rs, cm05)
                g = blk * 4 + t
                P.op("pool", lambda e, s=s, g=g: e.dma_start(out=x_out[g * 128:(g + 1) * 128, :], in_=xs[:, s, :]),
                     reads=[("xs", s)], dsem=("xst", s))
                if ag is not None:
                    P.op("act", lambda e, s=s: e.activation(out=xb[:], in_=xs[:, s, :], func=AF.Copy),
                         reads=[("xs", s)], writes=["xb"])
                    P.op("sp", lambda e, blk=blk, t=t: e.dma_start(out=ag[0][blk][t * 128:(t + 1) * 128, :], in_=xb[:]),
                         reads=["xb"], writes=[("agin", blk, t)], dsem="xbst")
                load_tile(g + NS)
            if ag is not None:
                P.op("pool", lambda e, blk=blk: e.collective_compute(
                    "AllGather", ALU.bypass, replica_groups=RG, ins=[ag[0][blk].opt()], outs=[ag[1][blk].opt()]),
                    reads=[("agin", blk, t) for t in range(4)], writes=[("agout", blk)], dsem="cc", dinc=1)
        P.emit()


def mixln_phase(nc, o_in, w_o, rsin, rsout, x_in, x_out, g_d, b_d, cst):
    with ExitStack() as es:
        tg = _tag()
        sb = lambda n, s, d: es.enter_context(nc.sbuf_tensor(n + tg, s, d))
        ps = lambda n, s, d: es.enter_context(nc.psum_tensor(n + tg, s, d))
        wo = sb("wo", [128, 4, D], BF16)
        stage = sb("stage", [128, 2, D], F32)
        os_ = sb("os", [128, 3, 512], F32)
        oT = sb("oT", [128, 2, 4, 128], BF16)
        ys = sb("ys", [128, 2, D], F32)
        ident = sb("ident", [128, 128], F32)
        xs = sb("xs", [128, 3, D], F32)
        ms = sb("ms", [128, 3, D], F32)
        gam = sb("gam", [128, D], F32)
        bet = sb("bet", [128, D], F32)
        st6 = sb("st6", [128, 2, 2, 6], F32)
        mv = sb("mv", [128, 2, 2], F32)
        rs = sb("rs", [128, 2], F32)
        cm05 = sb("cm05", [128, 1], F32)
        psT = ps("psT", [128, 2, 512], F32)
        psY = ps("psY", [128, 2, 512], F32)
        P = Prog(nc)
        P.op("pool", lambda e: e.memset(cm05[:], -0.5), writes=["cm05"])
        P.op("sp", lambda e: e.dma_start(out=ident[:], in_=cst["ident"][:, :]), writes=["ident"], dsem="c0")
        P.op("sp", lambda e: e.dma_start(out=gam[:], in_=g_d.partition_broadcast(128)), writes=["gam"], dsem="c1")
        P.op("sp", lambda e: e.dma_start(out=bet[:], in_=b_d.partition_broadcast(128)), writes=["bet"], dsem="c2")
        order = [(c, rr, j) for c in range(4) for rr in range(2) for j in range(4)]

        def load_tile(i):
            if i >= len(order):
                return
            c, rr, j = order[i]
            g = rr * 16 + c * 4 + j
            s = i % 3
            P.op("sp", lambda e: e.dma_start(out=os_[:, s, :], in_=o_in[g * 128:(g + 1) * 128, :]),
                 writes=[("os", s)], dsem=("ol", s))

        for i in range(3):
            load_tile(i)
        load_cast_weight(P, "dve", "wo", w_o, 4, D, stage, Rot(2), wo, D)
        rT, rY = Rot(2), Rot(2)

        def tile_body(i):
            c, rr, j = order[i]
            s, so = i % 3, i % 2
            b = rT.next()
            for kc in range(4):
                P.op("pe", lambda e, kc=kc: e.transpose(
                    out=psT[:, b, kc * 128:(kc + 1) * 128], in_=os_[:, s, kc * 128:(kc + 1) * 128], identity=ident[:]),
                    reads=[("os", s), "ident"], writes=[("psT", b)])
            P.op("act", lambda e: e.activation(out=oT[:, so, :, :], in_=psT[:, b, :].rearrange("p (j t) -> p j t", j=4), func=AF.Copy),
                 reads=[("psT", b)], writes=[("oT", so)])
            load_tile(i + 3)
            for h in range(2):
                by = rY.next()
                for kc in range(4):
                    P.op("pe", lambda e, kc=kc, h=h, by=by: e.matmul(
                        psY[:, by, :], lhsT=oT[:, so, kc, :], rhs=wo[:, kc, h * 512:(h + 1) * 512], start=(kc == 0), stop=(kc == 3)),
                        reads=[("oT", so), ("wo", kc)], writes=[("psY", by)])
                P.op("act", lambda e, by=by, h=h: e.activation(out=ys[:, so, h * 512:(h + 1) * 512], in_=psY[:, by, :], func=AF.Copy),
                     reads=[("psY", by)], writes=[("ys", so, h)])
            row = rr * 512 + j * 128
            P.op("sp", lambda e: e.dma_start(out=rsin[c][row:row + 128, :], in_=ys[:, so, :]),
                 reads=[("ys", so, 0), ("ys", so, 1)], writes=[("rsin", c, rr, j)], dsem=("yst", so))
            if rr == 1 and j == 3:
                P.op("pool", lambda e: e.collective_compute(
                    "ReduceScatter", ALU.add, replica_groups=RG, ins=[rsin[c].opt()], outs=[rsout[c].opt()]),
                    reads=[("rsin", c, a, b_) for a in range(2) for b_ in range(4)], writes=[("rsout", c)], dsem="cc", dinc=1)

        def ln_load(g):
            s = g % 3
            P.op("pool", lambda e: e.dma_start(out=xs[:, s, :], in_=x_in[g * 128:(g + 1) * 128, :]),
                 writes=[("xs", s)], dsem=("xl", s))
            P.op("pool", lambda e: e.dma_start(out=ms[:, s, :], in_=rsout[g // 4][(g % 4) * 128:(g % 4 + 1) * 128, :]),
                 reads=[("rsout", g // 4)], writes=[("ms", s)], dsem=("ml", s))

        def ln_tile_body(g):
            s = g % 3
            P.op("dve", lambda e: e.scalar_tensor_tensor(out=xs[:, s, :], in0=ms[:, s, :], scalar=1.0 / ALPHA, in1=xs[:, s, :],
                                                         op0=ALU.mult, op1=ALU.add),
                 reads=[("ms", s), ("xs", s)], writes=[("xs", s)])
            ln_tile(P, xs, ("xs", s), s, g % 2, gam, bet, LN_EPS / ALPHA ** 2, st6, mv, rs, cm05)
            P.op("pool", lambda e: e.dma_start(out=x_out[g * 128:(g + 1) * 128, :], in_=xs[:, s, :]),
                 reads=[("xs", s)], dsem=("xst", s))

        def ln_chunk(c):
            for g in range(4 * c, 4 * c + 3):
                ln_load(g)
            for g in range(4 * c, 4 * c + 4):
                ln_tile_body(g)
                if g + 3 < 4 * c + 4:
                    ln_load(g + 3)

        for c in range(4):
            for i in range(8 * c, 8 * c + 8):
                tile_body(i)
            if c >= 1:
                ln_chunk(c - 1)
        ln_chunk(3)
        P.emit()


def attn_phase(nc, T, kind, xsrc, o_out, ocol0, w_q, gn_d, cst, dl_d=None, lambda_init=0.0, dbg=None):
    HW = NH * 64
    NT, NB = T // 128, T // 512
    isA = kind == "A"
    dvw = 65 if isA else 64
    qscale = 32 ** -0.5 if isA else 0.125
    with ExitStack() as es:
        tg = _tag()
        sb = lambda n, s, d: es.enter_context(nc.sbuf_tensor(n + tg, s, d))
        ps = lambda n, s, d: es.enter_context(nc.psum_tensor(n + tg, s, d))
        wq = sb("wq", [128, 8, 3 * HW], BF16)
        stage = sb("stage", [128, 2, 3 * HW], F32)
        xs = sb("xs", [128, 5, D], BF16)
        xT = sb("xT", [128, 8, 512], BF16)
        qT = sb("qT", [64, NH, T], BF16)
        kT = sb("kT", [64, NH, T], BF16)
        vv = sb("vv", [128, NT, NH, dvw], BF16)
        ident = sb("ident", [128, 128], BF16)
        msk = sb("msk", [128, 4, 512], F32)
        pT = sb("pT", [128, 3, 512], BF16)
        oacc = sb("oacc", [128, NT, NH, 64], F32)
        ssq = sb("ssq", [128, NT * NH], F32)
        junk = sb("junk", [128, 64], F32)
        oTs = sb("oTs", [dvw, 2, 512], F32)
        identf = sb("identf", [128, 128], F32)
        gn = sb("gn", [128, 64], F32)
        rec = sb("rec", [128, 2, 4], F32)
        if isA:
            dl = sb("dl", [128, 4, 32], F32)
            dlp = sb("dlp", [128, 2, 32], F32)
            dls = sb("dls", [128, 2], F32)
            negl = sb("negl", [128, 1], F32)
        else:
            e32 = sb("e32", [128, 2, 512], F32)
            lp = sb("lp", [128, 3, 512], F32)
            lpacc = sb("lpacc", [128, 2, 512], F32)
            negU = sb("negU", [128, 128], F32)
            negO = sb("negO", [128, 128], F32)
        psX = ps("psX", [128, 1, 1024], BF16)
        psA = ps("psA", [128, 7, 512], F32)

        P = Prog(nc)
        P.op("sp", lambda e: e.dma_start(out=ident[:], in_=cst["identb"][:, :]), writes=["ident"], dsem="c0")
        P.op("sp", lambda e: e.dma_start(out=identf[:], in_=cst["ident"][:, :]), writes=["identf"], dsem="c9")
        P.op("sp", lambda e: e.dma_start(out=msk[:], in_=cst["maskA" if isA else "maskB"][:, :, :]), writes=["msk"], dsem="c1")
        P.op("sp", lambda e: e.dma_start(out=gn[:], in_=gn_d.partition_broadcast(128)), writes=["gn"], dsem="c2")
        if isA:
            P.op("sp", lambda e: e.dma_start(out=dl[:], in_=dl_d.partition_broadcast(128)), writes=["dl"], dsem="c3")
            P.op("dve", lambda e: e.tensor_tensor(out=dlp[:], in0=dl[:, 0:4:2, :], in1=dl[:, 1:4:2, :], op=ALU.mult),
                 reads=["dl"], writes=["dlp"])
            P.op("dve", lambda e: e.reduce_sum(out=dls[:], in_=dlp[:], axis=mybir.AxisListType.X),
                 reads=["dlp"], writes=["dls"])
            P.op("act", lambda e: e.activation(out=dls[:], in_=dls[:], func=AF.Exp), reads=["dls"], writes=["dls"])
            P.op("dve", lambda e: e.tensor_tensor(out=negl[:], in0=dls[:, 1:2], in1=dls[:, 0:1], op=ALU.subtract),
                 reads=["dls"], writes=["negl"])
            P.op("dve", lambda e: e.tensor_scalar(negl[:], negl[:], -lambda_init, None, ALU.add),
                 reads=["negl"], writes=["negl"])
            P.op("pool", lambda e: e.memset(vv[:, :, :, 64:65], 1.0), writes=["vones"])
        else:
            P.op("sp", lambda e: e.dma_start(out=negU[:], in_=cst["negU"][:, :]), writes=["negU"], dsem="c4")
            P.op("sp", lambda e: e.dma_start(out=negO[:], in_=cst["negO"][:, :]), writes=["negO"], dsem="c5")

        def load_tile(g):
            if g >= NT:
                return
            s = g % 5
            P.op("sp", lambda e: e.dma_start(out=xs[:, s, :], in_=xsrc(g)),
                 writes=[("xs", s)], dsem=("xl", s))

        for g in range(5):
            load_tile(g)
        load_cast_weight(P, "pool", "wq", w_q, 8, 3 * HW, stage, Rot(2), wq, 3 * HW)

        rT = Rot(2)
        for blk in range(NB):
            sl = [(blk * 4 + t) % 5 for t in range(4)]
            for kc in range(8):
                for t in range(4):
                    P.op("pe", lambda e, t=t, kc=kc, s=sl[t]: e.transpose(
                        out=psX[:, 0, t * 128:(t + 1) * 128], in_=xs[:, s, kc * 128:(kc + 1) * 128], identity=ident[:]),
                        reads=[("xs", sl[t]), "ident"], writes=["psX"])
                P.op("act" if kc % 2 else "dve", lambda e, kc=kc: e.tensor_copy(out=xT[:, kc, :], in_=psX[:, 0, 0:512]) if kc % 2 == 0 else
                     e.activation(out=xT[:, kc, :], in_=psX[:, 0, 0:512], func=AF.Copy),
                     reads=["psX"], writes=[("xT", kc)])
            for t in range(4):
                load_tile(blk * 4 + t + 5)
            for which, dst, sc, nm in ((0, qT, qscale, "qT"), (HW, kT, 1.0, "kT")):
                for h in range(NH):
                    b = rT.next()
                    for kc in range(8):
                        P.op("pe", lambda e, b=b, kc=kc, c=which + h * 64: e.matmul(
                            psA[0:64, b, :], lhsT=wq[:, kc, c:c + 64], rhs=xT[:, kc, :], start=(kc == 0), stop=(kc == 7)),
                            reads=[("wq", kc), ("xT", kc)], writes=[("psA", b)])
                    P.op("act", lambda e, b=b, h=h, dst=dst, sc=sc, blk=blk: e.activation(
                        out=dst[:, h, blk * 512:(blk + 1) * 512], in_=psA[0:64, b, :], func=AF.Copy, scale=sc),
                        reads=[("psA", b)], writes=[(nm, h, blk)])
            for t in range(4):
                b = rT.next()
                g = blk * 4 + t
                for kc in range(8):
                    P.op("pe", lambda e, b=b, kc=kc, t=t: e.matmul(
                        psA[:, b, 0:HW], lhsT=xT[:, kc, t * 128:(t + 1) * 128], rhs=wq[:, kc, 2 * HW:3 * HW],
                        start=(kc == 0), stop=(kc == 7)),
                        reads=[("wq", kc), ("xT", kc)], writes=[("psA", b)])
                P.op("dve", lambda e, b=b, g=g: e.tensor_copy(
                    out=vv[:, g, :, 0:64], in_=psA[:, b, 0:HW].rearrange("p (h d) -> p h d", h=NH)),
                    reads=[("psA", b)], writes=[("vv", g)])

        SBK = [0, 1, 2] if isA else [0, 1]
        WBK = [2, 3]
        OBK = [3, 4] if isA else [4, 5]
        FBK = [5, 6] if isA else [6]
        rS, rW, rO, rP, rE, rL = Rot(len(SBK)), Rot(len(WBK)), Rot(2), Rot(3), Rot(2), Rot(3)
        rF, rOT = Rot(len(FBK)), Rot(2)
        steps = []
        for J in range(NB):
            for h in range(NH):
                for i in range(2 if isA else 1):
                    ks = list(range(0, 4 * J + 4)) if isA else list(range(4 * J + 3, -1, -1))
                    grp = {}
                    for n_, kb in enumerate(ks):
                        steps.append(dict(J=J, h=h, i=i, kb=kb, r=kb - 4 * J, first=(n_ == 0), last=(n_ == len(ks) - 1), grp=grp))
        la_state = [0]

        def stage_scores(st):
            J, h, i, kb, r = st["J"], st["h"], st["i"], st["kb"], st["r"]
            if st["first"]:
                st["grp"]["bo"] = OBK[rO.next()]
                st["grp"]["pvfirst"] = True
            bs = SBK[rS.next()]
            st["bs"] = bs
            if isA:
                lk, rq = kT[32 * i:32 * i + 32, h, kb * 128:(kb + 1) * 128], qT[32 * i:32 * i + 32, h, J * 512:(J + 1) * 512]
            else:
                lk, rq = kT[:, h, kb * 128:(kb + 1) * 128], qT[:, h, J * 512:(J + 1) * 512]
            st["lk"], st["rq"] = lk, rq
            P.op("pe", lambda e: e.matmul(psA[:, bs, :], lhsT=lk, rhs=rq, start=True, stop=True),
                 reads=[("kT", h, kb // 4), ("qT", h, J)], writes=[("psA", bs)])
            if isA:
                sp_ = rP.next()
                st["sp"] = sp_
                P.op("act", lambda e: e.activation(out=pT[:, sp_, :], in_=psA[:, bs, :], func=AF.Exp),
                     reads=[("psA", bs)], writes=[("pT", sp_)])
                if r >= 0:
                    P.op("pool", lambda e: e.tensor_tensor(out=pT[:, sp_, :], in0=pT[:, sp_, :], in1=msk[:, r, :], op=ALU.mult),
                         reads=[("pT", sp_), "msk"], writes=[("pT", sp_)])
            else:
                se, sl_ = rE.next(), rL.next()
                st["sl"] = sl_
                bw = WBK[rW.next()]
                st["bw"] = bw
                P.op("pe", lambda e: e.matmul(psA[:, bw, :], lhsT=lk, rhs=rq, start=True, stop=False),
                     reads=[("kT", h, kb // 4), ("qT", h, J)], writes=[("psA", bw)])
                P.op("act", lambda e: e.activation(out=e32[:, se, :], in_=psA[:, bs, :], func=AF.Exp),
                     reads=[("psA", bs)], writes=[("e32", se)])
                P.op("act", lambda e: e.activation(out=lp[:, sl_, :], in_=e32[:, se, :], func=AF.Ln, bias=1.0),
                     reads=[("e32", se)], writes=[("lp", sl_)])
                if r >= 0:
                    P.op("pool", lambda e: e.tensor_tensor(out=lp[:, sl_, :], in0=lp[:, sl_, :], in1=msk[:, r, :], op=ALU.mult),
                         reads=[("lp", sl_), "msk"], writes=[("lp", sl_)])

        def stage_cumsum(st):
            r, bw, sl_ = st["r"], st["bw"], st["sl"]
            first, last = st["first"], st["last"]
            la = la_state[0]
            P.op("pe", lambda e: e.matmul(psA[:, bw, :], lhsT=negU[:], rhs=lp[:, sl_, :], start=False, stop=first),
                 reads=["negU", ("lp", sl_)], writes=[("psA", bw)])
            if not first:
                P.op("pe", lambda e: e.matmul(psA[:, bw, :], lhsT=negO[:], rhs=lpacc[:, la, :], start=False, stop=True),
                     reads=["negO", ("lpacc", la)], writes=[("psA", bw)])
            sp_ = rP.next()
            st["sp"] = sp_
            P.op("act", lambda e: e.activation(out=pT[:, sp_, :], in_=psA[:, bw, :], func=AF.Exp),
                 reads=[("psA", bw)], writes=[("pT", sp_)])
            if r >= 0:
                P.op("pool", lambda e: e.tensor_tensor(out=pT[:, sp_, :], in0=pT[:, sp_, :], in1=msk[:, r, :], op=ALU.mult),
                     reads=[("pT", sp_), "msk"], writes=[("pT", sp_)])
            if not last:
                if first:
                    P.op("dve", lambda e: e.tensor_copy(out=lpacc[:, 1 - la, :], in_=lp[:, sl_, :]),
                         reads=[("lp", sl_)], writes=[("lpacc", 1 - la)])
                else:
                    P.op("dve", lambda e: e.tensor_tensor(out=lpacc[:, 1 - la, :], in0=lpacc[:, la, :], in1=lp[:, sl_, :], op=ALU.add),
                         reads=[("lpacc", la), ("lp", sl_)], writes=[("lpacc", 1 - la)])
                la_state[0] = 1 - la

        def stage_pv(st):
            J, h, i, kb, r, sp_ = st["J"], st["h"], st["i"], st["kb"], st["r"], st["sp"]
            grp = st["grp"]
            bo = grp["bo"]
            first = grp["pvfirst"]
            grp["pvfirst"] = False
            P.op("pe", lambda e: e.matmul(psA[0:dvw, bo, :], lhsT=vv[:, kb, h, 0:dvw], rhs=pT[:, sp_, :],
                                          start=first, stop=st["last"]),
                 reads=[("pT", sp_), ("vv", kb)] + (["vones"] if isA else []), writes=[("psA", bo)])
            if not st["last"]:
                return
            so_ = rOT.next()
            P.op("act", lambda e: e.activation(out=oTs[:, so_, :], in_=psA[0:dvw, bo, :], func=AF.Copy),
                 reads=[("psA", bo)], writes=[("oTs", so_)])
            bf = FBK[rF.next()]
            for c in range(4):
                P.op("pe", lambda e, c=c: e.transpose(out=psA[:, bf, c * dvw:(c + 1) * dvw], in_=oTs[:, so_, c * 128:(c + 1) * 128],
                                                      identity=identf[0:dvw, 0:dvw]),
                     reads=[("oTs", so_), "identf"], writes=[("psA", bf)])
            if isA:
                P.op("dve", lambda e: e.reciprocal(out=rec[:, i, :], in_=psA[:, bf, 64:260:65]),
                     reads=[("psA", bf)], writes=[("rec", i)])
                if i == 1:
                    P.op("dve", lambda e: e.tensor_scalar(rec[:, 1, :], rec[:, 1, :], negl[:, 0:1], None, ALU.mult),
                         reads=[("rec", 1), "negl"], writes=[("rec", 1)])
                for c in range(4):
                    g = J * 4 + c
                    if i == 0:
                        P.op("dve", lambda e, c=c, g=g: e.tensor_scalar(
                            oacc[:, g, h, :], psA[:, bf, c * 65:c * 65 + 64], rec[:, 0, c:c + 1], None, ALU.mult),
                            reads=[("psA", bf), ("rec", 0)], writes=[("oacc", g, h)])
                    else:
                        P.op("dve", lambda e, c=c, g=g: e.scalar_tensor_tensor(
                            out=oacc[:, g, h, :], in0=psA[:, bf, c * 65:c * 65 + 64], scalar=rec[:, 1, c:c + 1],
                            in1=oacc[:, g, h, :], op0=ALU.mult, op1=ALU.add),
                            reads=[("psA", bf), ("rec", 1), ("oacc", g, h)], writes=[("oacc", g, h)])
            else:
                for c in range(4):
                    g = J * 4 + c
                    P.op("dve", lambda e, c=c, g=g: e.tensor_copy(out=oacc[:, g, h, :], in_=psA[:, bf, c * 64:(c + 1) * 64]),
                         reads=[("psA", bf)], writes=[("oacc", g, h)])
            if (not isA) or i == 1:
                for c in range(4):
                    g = J * 4 + c
                    P.op("act", lambda e, g=g: e.activation(
                        out=junk[:], in_=oacc[:, g, h, :], func=AF.Square, accum_out=ssq[:, g * NH + h:g * NH + h + 1]),
                        reads=[("oacc", g, h)], writes=["junk", ("ssq", g, h)])

        N = len(steps)
        if isA:
            for n in range(N + 1):
                if n < N:
                    stage_scores(steps[n])
                if n >= 1:
                    stage_pv(steps[n - 1])
        else:
            for n in range(N + 2):
                if n < N:
                    stage_scores(steps[n])
                if 1 <= n <= N:
                    stage_cumsum(steps[n - 1])
                if n >= 2:
                    stage_pv(steps[n - 2])
        allssq = [("ssq", g, h) for g in range(NT) for h in range(NH)]
        P.op("act", lambda e: e.activation(out=ssq[:], in_=ssq[:], func=AF.Sqrt, bias=RMS_EPS, scale=1.0 / 64),
             reads=allssq, writes=["rs0"])
        P.op("dve", lambda e: e.reciprocal(out=ssq[:], in_=ssq[:]), reads=["rs0"], writes=["rs1"])
        if isA:
            P.op("dve", lambda e: e.tensor_scalar(ssq[:], ssq[:], 1.0 - lambda_init, None, ALU.mult),
                 reads=["rs1"], writes=["rs1"])
        for g in range(NT):
            for h in range(NH):
                P.op("dve", lambda e, g=g, h=h: e.scalar_tensor_tensor(
                    out=oacc[:, g, h, :], in0=oacc[:, g, h, :], scalar=ssq[:, g * NH + h:g * NH + h + 1],
                    in1=gn[:], op0=ALU.mult, op1=ALU.mult),
                    reads=[("oacc", g, h), "rs1", "gn"], writes=[("oacc", g, h)])
            P.op("sp", lambda e, g=g: e.dma_start(
                out=o_out[g * 128:(g + 1) * 128, ocol0:ocol0 + HW], in_=oacc[:, g, :, :].rearrange("p h d -> p (h d)")),
                reads=[("oacc", g, h) for h in range(NH)], dsem=("ost", g % 4))
        P.emit()


W_GB, W_DM, W_DMI, W_NDMS, W_N, W_NT, W_IN, W_INT, W_EG, W_QG = range(10)
W_MA, W_MB, W_MTA, W_MTB, W_PTA, W_PTB, W_KBG, W_KD, W_VB, W_U, W_WT, W_VN, W_GR = range(10, 23)
NTMP = 23


def gdn_phase(nc, T, xsrc, o_out, ocol0, w_c, cw_d, alog_d, dtb_d, gn_d, cst):
    NT, NB = T // 128, T // 512
    NS = 4
    NCH = 3 * NH
    ZO = NCH * 128
    BO = ZO + NH * 128
    WC = BO + 2 * NH
    HD = NH * 128
    with ExitStack() as es:
        tg = _tag()
        sb = lambda n, s, d: es.enter_context(nc.sbuf_tensor(n + tg, s, d))
        ps = lambda n, s, d: es.enter_context(nc.psum_tensor(n + tg, s, d))
        wC = sb("wC", [128, 8, WC], BF16)
        stage = sb("stage", [128, 2, WC], F32)
        xs = sb("xs", [128, NS, D], BF16)
        identb = sb("identb", [128, 128], BF16)
        sq2 = sb("sq2", [128, 4, HD], F32)
        xT = sb("xT", [128, 8, 512], BF16)
        pc = sb("pc", [128, NCH, 515], F32)
        csA = sb("cs", [128, 2, NCH, 512], F32)
        sq = sb("sq", [128, 4, 512], F32)
        ktA = sb("kt", [128, 2, 4, NH, 128], F32)
        vtA = sb("vt", [128, 2, 4, NH, 128], F32)
        zsA = sb("zs", [128, 2, 4, HD], F32)
        ba = sb("ba", [128, 4, 2 * NH], F32)
        betaA = sb("beta", [128, 2, 4, NH], F32)
        ggA = sb("gg", [128, 2, 4, NH], F32)
        dtb = sb("dtb", [128, 4, NH], F32)
        nega = sb("nega", [128, 4, NH], F32)
        cw = sb("cw", [128, NCH, 4], F32)
        ident = sb("ident", [128, 128], F32)
        onesM = sb("onesM", [128, 128], F32)
        triU = sb("triU", [128, 128], F32)
        maskLI = sb("maskLI", [128, 128], F32)
        negLS = sb("negLS", [128, 128], F32)
        epsb = sb("epsb", [128, 1], F32)
        gnC = sb("gnC", [128, 128], F32)
        S = sb("S", [128, NH, 128], F32)
        gcum = sb("gcum", [128, 4, NH], F32)
        egc = sb("egc", [128, 4, NH], F32)
        bg4 = sb("bg4", [128, 4, NH], F32)
        sm = sb("sm", [128, 6, 4], F32)
        Wt = sb("Wt", [128, 6, NTMP, 128], F32)
        obuf = sb("obuf", [128, 4, HD], F32)
        ssq = sb("ssq", [128, 4 * NH], F32)
        psX = ps("psX", [128, 2, 1024], BF16)
        psB = ps("psB", [128, 6, 512], F32)

        P = Prog(nc)
        ci = [0]

        def cload(dst, src, key):
            ci[0] += 1
            P.op("sp", lambda e: e.dma_start(out=dst, in_=src), writes=[key], dsem=("c", ci[0]))

        cload(ident[:], cst["ident"][:, :], "ident")
        cload(identb[:], cst["identb"][:, :], "identb")
        cload(onesM[:], cst["onesM"][:, :], "onesM")
        cload(triU[:], cst["triU"][:, :], "triU")
        cload(maskLI[:], cst["maskLI"][:, :], "maskLI")
        cload(negLS[:], cst["negLS"][:, :], "negLS")
        cload(cw[:], cw_d[:, :, :], "cw")
        cload(gnC[:], gn_d.partition_broadcast(128), "gnC")
        for t in range(4):
            cload(dtb[:, t, :], dtb_d.partition_broadcast(128), ("dtb", t))
            cload(nega[:, t, :], alog_d.partition_broadcast(128), ("nega", t))
        P.op("act", lambda e: e.activation(out=nega[:], in_=nega[:], func=AF.Exp),
             reads=[("nega", t) for t in range(4)], writes=["nega"])
        P.op("dve", lambda e: e.tensor_scalar(nega[:], nega[:], -1.0, None, ALU.mult), reads=["nega"], writes=["nega"])
        P.op("pool", lambda e: e.memset(epsb[:], RMS_EPS), writes=["epsb"])
        P.op("pool", lambda e: e.memset(S[:], 0.0), writes=[("S", h) for h in range(NH)])
        P.op("pool", lambda e: e.memset(pc[:, :, 0:3], 0.0), writes=[("pc", ch) for ch in range(NCH)])

        def load_tile(g):
            if g >= NT:
                return
            s = g % NS
            P.op("sp", lambda e: e.dma_start(out=xs[:, s, :], in_=xsrc(g)),
                 writes=[("xs", s)], dsem=("xl", s))

        for g in range(NS):
            load_tile(g)
        load_cast_weight(P, "pool", "wC", w_c, 8, WC, stage, Rot(2), wC, WC)

        rT, rN, rE, rX = Rot(6), Rot(4), Rot(2), Rot(2)
        Wk = lambda ss, i: Wt[:, ss, i, :]

        def qslot():
            q = rT.next()
            return q, psB[:, q, 0:128]

        def prep_gen(blk):
            pb = blk % 2
            cs, kt, vt, zs, beta, gg = csA[:, pb], ktA[:, pb], vtA[:, pb], zsA[:, pb], betaA[:, pb], ggA[:, pb]
            sl = [(blk * 4 + t) % NS for t in range(4)]
            for kc in range(8):
                bx = rX.next()
                for t in range(4):
                    P.op("pe", lambda e, t=t, kc=kc, s=sl[t], bx=bx: e.transpose(
                        out=psX[:, bx, t * 128:(t + 1) * 128], in_=xs[:, s, kc * 128:(kc + 1) * 128], identity=identb[:]),
                        reads=[("xs", sl[t]), "identb"], writes=[("psX", bx)])
                P.op("dve", lambda e, kc=kc, bx=bx: e.tensor_copy(out=xT[:, kc, :], in_=psX[:, bx, 0:512]),
                     reads=[("psX", bx)], writes=[("xT", kc)])
                if kc % 2:
                    yield
            for t in range(4):
                load_tile(blk * 4 + t + NS)
            for ch in range(NCH):
                b = rT.next()
                for kc in range(8):
                    P.op("pe", lambda e, b=b, kc=kc, ch=ch: e.matmul(
                        psB[:, b, :], lhsT=wC[:, kc, ch * 128:(ch + 1) * 128], rhs=xT[:, kc, :], start=(kc == 0), stop=(kc == 7)),
                        reads=[("wC", kc), ("xT", kc)], writes=[("psB", b)])
                P.op("act", lambda e, b=b, ch=ch: e.activation(out=pc[:, ch, 3:515], in_=psB[:, b, :], func=AF.Copy),
                     reads=[("psB", b)], writes=[("pc", ch)])
                yield
            for t in range(4):
                b = rT.next()
                for kc in range(8):
                    P.op("pe", lambda e, b=b, kc=kc, t=t: e.matmul(
                        psB[:, b, 0:HD], lhsT=xT[:, kc, t * 128:(t + 1) * 128], rhs=wC[:, kc, ZO:BO], start=(kc == 0), stop=(kc == 7)),
                        reads=[("wC", kc), ("xT", kc)], writes=[("psB", b)])
                P.op("act", lambda e, b=b, t=t: e.activation(out=zs[:, t, :], in_=psB[:, b, 0:HD], func=AF.Silu),
                     reads=[("psB", b)], writes=[("zs", pb, t)])
                yield
            for t in range(4):
                q, qa = qslot()
                for kc in range(8):
                    P.op("pe", lambda e, qa=qa, kc=kc, t=t: e.matmul(
                        qa[:, 0:2 * NH], lhsT=xT[:, kc, t * 128:(t + 1) * 128], rhs=wC[:, kc, BO:WC], start=(kc == 0), stop=(kc == 7)),
                        reads=[("wC", kc), ("xT", kc)], writes=[("psB", q)])
                P.op("dve", lambda e, qa=qa, t=t: e.tensor_copy(out=ba[:, t, :], in_=qa[:, 0:2 * NH]),
                     reads=[("psB", q)], writes=[("ba", t)])
                yield
            for ch in range(NCH):
                P.op("dve", lambda e, ch=ch: e.tensor_scalar(cs[:, ch, :], pc[:, ch, 0:512], cw[:, ch, 0:1], None, ALU.mult),
                     reads=[("pc", ch), "cw"], writes=[("cs", pb, ch)])
                for i in range(1, 4):
                    P.op("dve", lambda e, ch=ch, i=i: e.scalar_tensor_tensor(
                        out=cs[:, ch, :], in0=pc[:, ch, i:i + 512], scalar=cw[:, ch, i:i + 1], in1=cs[:, ch, :],
                        op0=ALU.mult, op1=ALU.add),
                        reads=[("pc", ch), "cw", ("cs", pb, ch)], writes=[("cs", pb, ch)])
                P.op("act", lambda e, ch=ch: e.activation(out=cs[:, ch, :], in_=cs[:, ch, :], func=AF.Silu),
                     reads=[("cs", pb, ch)], writes=[("cs", pb, ch)])
                yield
            P.op("pool", lambda e: e.tensor_copy(out=pc[:, :, 0:3], in_=pc[:, :, 512:515]),
                 reads=[("pc", ch) for ch in range(NCH)], writes=[("pc", ch) for ch in range(NCH)])
            bak = [("ba", t) for t in range(4)]
            P.op("act", lambda e: e.activation(out=beta[:], in_=ba[:, :, 0:NH], func=AF.Exp, scale=-1.0), reads=bak, writes=[("beta", pb)])
            P.op("dve", lambda e: e.tensor_scalar(beta[:], beta[:], 1.0, None, ALU.add), reads=[("beta", pb)], writes=[("beta", pb)])
            P.op("dve", lambda e: e.reciprocal(out=beta[:], in_=beta[:]), reads=[("beta", pb)], writes=[("beta", pb)])
            P.op("dve", lambda e: e.tensor_tensor(out=gg[:], in0=ba[:, :, NH:2 * NH], in1=dtb[:], op=ALU.add),
                 reads=bak + [("dtb", t) for t in range(4)], writes=[("gg", pb)])
            P.op("act", lambda e: e.activation(out=gg[:], in_=gg[:], func=AF.Exp), reads=[("gg", pb)], writes=[("gg", pb)])
            P.op("act", lambda e: e.activation(out=gg[:], in_=gg[:], func=AF.Ln, bias=1.0), reads=[("gg", pb)], writes=[("gg", pb)])
            P.op("dve", lambda e: e.tensor_tensor(out=gg[:], in0=gg[:], in1=nega[:], op=ALU.mult), reads=[("gg", pb), "nega"], writes=[("gg", pb)])
            yield
            nsl = [rN.next() for _ in range(2 * NH)]
            nbk = []
            for ch in range(2 * NH):
                s2 = nsl[ch]
                P.op("act", lambda e, ch=ch, s2=s2: e.activation(out=sq[:, s2, :], in_=cs[:, ch, :], func=AF.Square),
                     reads=[("cs", pb, ch)], writes=[("sq", s2)])
            yield
            for ch in range(2 * NH):
                s2 = nsl[ch]
                bn = rT.next()
                nbk.append(bn)
                P.op("pe", lambda e, s2=s2, bn=bn: e.matmul(psB[:, bn, :], lhsT=onesM[:], rhs=sq[:, s2, :], start=True, stop=True),
                     reads=["onesM", ("sq", s2)], writes=[("psB", bn)])
                P.op("act", lambda e, s2=s2, bn=bn: e.activation(out=sq[:, s2, :], in_=psB[:, bn, :], func=AF.Ln, bias=epsb[:, 0:1]),
                     reads=[("psB", bn), "epsb"], writes=[("sq", s2)])
                yield
            for ch in range(2 * NH):
                s2 = nsl[ch]
                P.op("act", lambda e, s2=s2: e.activation(out=sq[:, s2, :], in_=sq[:, s2, :], func=AF.Exp, scale=-0.5),
                     reads=[("sq", s2)], writes=[("sq", s2)])
                if ch < NH:
                    P.op("dve", lambda e, ch=ch, s2=s2: e.scalar_tensor_tensor(
                        out=cs[:, ch, :], in0=cs[:, ch, :], scalar=128 ** -0.5, in1=sq[:, s2, :], op0=ALU.mult, op1=ALU.mult),
                        reads=[("cs", pb, ch), ("sq", s2)], writes=[("cs", pb, ch)])
                else:
                    P.op("dve", lambda e, ch=ch, s2=s2: e.tensor_tensor(out=cs[:, ch, :], in0=cs[:, ch, :], in1=sq[:, s2, :], op=ALU.mult),
                         reads=[("cs", pb, ch), ("sq", s2)], writes=[("cs", pb, ch)])
            yield
            for dst, nm, c0 in ((kt, "kt", NH), (vt, "vt", 2 * NH)):
                for hh in range(NH):
                    b = rT.next()
                    for t in range(4):
                        P.op("pe", lambda e, b=b, t=t, ch=c0 + hh: e.transpose(
                            out=psB[:, b, t * 128:(t + 1) * 128], in_=cs[:, ch, t * 128:(t + 1) * 128], identity=ident[:]),
                            reads=[("cs", pb, c0 + hh), "ident"], writes=[("psB", b)])
                    P.op("act", lambda e, b=b, hh=hh, dst=dst: e.activation(
                        out=dst[:, :, hh, :], in_=psB[:, b, :].rearrange("p (t d) -> p t d", t=4), func=AF.Copy),
                        reads=[("psB", b)], writes=[(nm, pb, hh)])
                    yield

        def run_chains(blk, extra):
            chains = [(t, h) for t in range(4) for h in range(NH)]
            active, idx = ([extra] if extra is not None else []), 0
            while idx < len(chains) or active:
                while len(active) < (7 if extra is not None else 6) and idx < len(chains):
                    t, h = chains[idx]
                    idx += 1
                    if h == 0:
                        chunk_body(blk, t)
                    active.append(head_body(blk, t, h, t))
                for gen in list(active):
                    try:
                        next(gen)
                    except StopIteration:
                        active.remove(gen)

        def epilogue(blk):
            pb = blk % 2
            zs = zsA[:, pb]
            P.op("act", lambda e: e.activation(out=sq2[:], in_=obuf[:], func=AF.Square),
                 reads=[("obuf", t, h) for t in range(4) for h in range(NH)], writes=["sq2"])
            P.op("dve", lambda e: e.reduce_sum(out=ssq[:], in_=sq2[:].rearrange("p t (h d) -> p (t h) d", h=NH),
                                               axis=mybir.AxisListType.X),
                 reads=["sq2"], writes=["ssq"])
            P.op("act", lambda e: e.activation(out=ssq[:], in_=ssq[:], func=AF.Ln, scale=1.0 / 128, bias=epsb[:, 0:1]),
                 reads=["ssq", "epsb"], writes=["ssq"])
            P.op("act", lambda e: e.activation(out=ssq[:], in_=ssq[:], func=AF.Exp, scale=-0.5), reads=["ssq"], writes=["ssq"])
            for t in range(4):
                for h in range(NH):
                    P.op("dve", lambda e, t=t, h=h: e.scalar_tensor_tensor(
                        out=obuf[:, t, h * 128:(h + 1) * 128], in0=obuf[:, t, h * 128:(h + 1) * 128],
                        scalar=ssq[:, t * NH + h:t * NH + h + 1], in1=gnC[:], op0=ALU.mult, op1=ALU.mult),
                        reads=[("obuf", t, h), "ssq", "gnC"], writes=[("obuf", t, h)])
                P.op("pool", lambda e, t=t: e.tensor_tensor(out=obuf[:, t, :], in0=obuf[:, t, :], in1=zs[:, t, :], op=ALU.mult),
                     reads=[("obuf", t, h) for h in range(NH)] + [("zs", pb, t)], writes=[("obuf", t, h) for h in range(NH)])
                g = blk * 4 + t
                P.op("sp", lambda e, t=t, g=g: e.dma_start(out=o_out[g * 128:(g + 1) * 128, ocol0:ocol0 + HD], in_=obuf[:, t, :]),
                     reads=[("obuf", t, h) for h in range(NH)], dsem=("ost", t))

        scan_done = [0] * NH

        def chunk_body(blk, t):
            pb = blk % 2
            beta, gg = betaA[:, pb], ggA[:, pb]
            cp = t
            q, qa = qslot()
            P.op("pe", lambda e: e.matmul(qa[:, 0:NH], lhsT=triU[:], rhs=gg[:, t, :], start=True, stop=True),
                 reads=["triU", ("gg", pb)], writes=[("psB", q)])
            P.op("dve", lambda e: e.tensor_copy(out=gcum[:, cp, :], in_=qa[:, 0:NH]), reads=[("psB", q)], writes=[("gcum", cp)])
            P.op("act", lambda e: e.activation(out=egc[:, cp, :], in_=gcum[:, cp, :], func=AF.Exp),
                 reads=[("gcum", cp)], writes=[("egc", cp)])
            P.op("dve", lambda e: e.tensor_tensor(out=bg4[:, cp, :], in0=beta[:, t, :], in1=egc[:, cp, :], op=ALU.mult),
                 reads=[("beta", pb), ("egc", cp)], writes=[("bg4", cp)])

        def head_body(blk, t, h, cp):
            pb = blk % 2
            cs, kt, vt, beta, gg = csA[:, pb], ktA[:, pb], vtA[:, pb], betaA[:, pb], ggA[:, pb]
            ss = (t * NH + h) % 6
            tk = lambda i: ("W", ss, i)
            tsl = slice(t * 128, (t + 1) * 128)
            qTa, kTa = cs[:, h, tsl], cs[:, NH + h, tsl]
            bcol = beta[:, t, h:h + 1]
            gcol = gcum[:, cp, h:h + 1]
            P.op("act", lambda e: e.activation(out=Wk(ss, W_GB), in_=onesM[:], func=AF.Copy, scale=gg[:, t, h:h + 1]),
                 reads=["onesM", ("gg", pb)], writes=[tk(W_GB)])
            q1, grow = qslot()
            P.op("pe", lambda e: e.matmul(grow, lhsT=Wk(ss, W_GB), rhs=triU[:], start=True, stop=True),
                 reads=[tk(W_GB), "triU"], writes=[("psB", q1)])
            P.op("dve", lambda e: e.tensor_copy(out=Wk(ss, W_GR), in_=grow), reads=[("psB", q1)], writes=[tk(W_GR)])
            P.op("act", lambda e: e.activation(out=Wk(ss, W_EG), in_=Wk(ss, W_GR), func=AF.Exp), reads=[tk(W_GR)], writes=[tk(W_EG)])
            P.op("act", lambda e: e.activation(out=sm[:, ss, 1:2], in_=Wt[:, ss, W_GR, 127:128], func=AF.Exp),
                 reads=[tk(W_GR)], writes=[("sm", ss, 1)])
            yield
            P.op("dve", lambda e: e.tensor_scalar(Wk(ss, W_DM), Wk(ss, W_GR), gcol, 0.0, ALU.subtract, ALU.max),
                 reads=[tk(W_GR), ("gcum", cp)], writes=[tk(W_DM)])
            P.op("act", lambda e: e.activation(out=Wk(ss, W_DM), in_=Wk(ss, W_DM), func=AF.Exp, scale=-1.0),
                 reads=[tk(W_DM)], writes=[tk(W_DM)])
            P.op("dve", lambda e: e.tensor_tensor(out=Wk(ss, W_DMI), in0=Wk(ss, W_DM), in1=maskLI[:], op=ALU.mult),
                 reads=[tk(W_DM), "maskLI"], writes=[tk(W_DMI)])
            P.op("pool", lambda e: e.tensor_tensor(out=Wk(ss, W_NDMS), in0=Wk(ss, W_DM), in1=negLS[:], op=ALU.mult),
                 reads=[tk(W_DM), "negLS"], writes=[tk(W_NDMS)])
            yield
            P.op("dve", lambda e: e.tensor_tensor(out=sm[:, ss, 2:3], in0=Wt[:, ss, W_GR, 127:128], in1=gcol, op=ALU.subtract),
                 reads=[tk(W_GR), ("gcum", cp)], writes=[("sm", ss, 2)])
            P.op("act", lambda e: e.activation(out=sm[:, ss, 3:4], in_=sm[:, ss, 2:3], func=AF.Exp),
                 reads=[("sm", ss, 2)], writes=[("sm", ss, 3)])
            yield
            P.op("act", lambda e: e.activation(out=Wk(ss, W_KBG), in_=kt[:, t, h, :], func=AF.Copy, scale=bg4[:, cp, h:h + 1]),
                 reads=[("kt", pb, h), ("bg4", cp)], writes=[tk(W_KBG)])
            P.op("act", lambda e: e.activation(out=Wk(ss, W_KD), in_=kt[:, t, h, :], func=AF.Copy, scale=sm[:, ss, 3:4]),
                 reads=[("kt", pb, h), ("sm", ss, 3)], writes=[tk(W_KD)])
            P.op("act", lambda e: e.activation(out=Wk(ss, W_VB), in_=vt[:, t, h, :], func=AF.Copy, scale=bcol),
                 reads=[("vt", pb, h), ("beta", pb)], writes=[tk(W_VB)])
            yield
            q2, gk = qslot()
            P.op("pe", lambda e: e.matmul(gk, lhsT=kTa, rhs=kTa, start=True, stop=True),
                 reads=[("cs", pb, NH + h)], writes=[("psB", q2)])
            P.op("dve", lambda e: e.scalar_tensor_tensor(out=Wk(ss, W_N), in0=gk, scalar=bcol, in1=Wk(ss, W_NDMS),
                                                         op0=ALU.mult, op1=ALU.mult),
                 reads=[("psB", q2), ("beta", pb), tk(W_NDMS)], writes=[tk(W_N)])
            yield
            q3, ntp = qslot()
            P.op("pe", lambda e: e.transpose(out=ntp, in_=Wk(ss, W_N), identity=ident[:]),
                 reads=[tk(W_N), "ident"], writes=[("psB", q3)])
            P.op("act", lambda e: e.activation(out=Wk(ss, W_NT), in_=ntp, func=AF.Copy), reads=[("psB", q3)], writes=[tk(W_NT)])
            yield
            q4, qk = qslot()
            P.op("pe", lambda e: e.matmul(qk, lhsT=qTa, rhs=kTa, start=True, stop=True),
                 reads=[("cs", pb, h), ("cs", pb, NH + h)], writes=[("psB", q4)])
            P.op("dve", lambda e: e.tensor_tensor(out=Wk(ss, W_IN), in0=qk, in1=Wk(ss, W_DMI), op=ALU.mult),
                 reads=[("psB", q4), tk(W_DMI)], writes=[tk(W_IN)])
            yield
            q5, itp = qslot()
            P.op("pe", lambda e: e.transpose(out=itp, in_=Wk(ss, W_IN), identity=ident[:]),
                 reads=[tk(W_IN), "ident"], writes=[("psB", q5)])
            P.op("act", lambda e: e.activation(out=Wk(ss, W_INT), in_=itp, func=AF.Copy), reads=[("psB", q5)], writes=[tk(W_INT)])
            yield
            P.op("pool", lambda e: e.tensor_tensor(out=Wk(ss, W_QG), in0=qTa, in1=Wk(ss, W_EG), op=ALU.mult),
                 reads=[("cs", pb, h), tk(W_EG)], writes=[tk(W_QG)])
            yield
            P.op("pool", lambda e: e.tensor_tensor(out=Wk(ss, W_PTA), in0=Wk(ss, W_NT), in1=ident[:], op=ALU.add),
                 reads=[tk(W_NT), "ident"], writes=[tk(W_PTA)])
            m_prev, mt_prev, pt_prev = W_N, W_NT, W_PTA
            for j in range(1, 7):
                m_new = W_MA if j % 2 else W_MB
                mt_new = W_MTA if j % 2 else W_MTB
                pt_new = W_PTB if j % 2 else W_PTA
                qm, pm = qslot()
                P.op("pe", lambda e, pm=pm, a=mt_prev, b_=m_prev: e.matmul(pm, lhsT=Wk(ss, a), rhs=Wk(ss, b_), start=True, stop=True),
                     reads=[tk(mt_prev), tk(m_prev)], writes=[("psB", qm)])
                P.op("act", lambda e, pm=pm, m_new=m_new: e.activation(out=Wk(ss, m_new), in_=pm, func=AF.Copy),
                     reads=[("psB", qm)], writes=[tk(m_new)])
                if j < 6:
                    qn, pn = qslot()
                    P.op("pe", lambda e, pn=pn, a=m_prev, b_=mt_prev: e.matmul(pn, lhsT=Wk(ss, a), rhs=Wk(ss, b_), start=True, stop=True),
                         reads=[tk(m_prev), tk(mt_prev)], writes=[("psB", qn)])
                    P.op("dve", lambda e, pn=pn, mt_new=mt_new: e.tensor_copy(out=Wk(ss, mt_new), in_=pn),
                         reads=[("psB", qn)], writes=[tk(mt_new)])
                yield
                qu, pu = qslot()
                P.op("pe", lambda e, pu=pu, m_new=m_new, pt_prev=pt_prev: e.matmul(pu, lhsT=Wk(ss, m_new), rhs=Wk(ss, pt_prev), start=True, stop=True),
                     reads=[tk(m_new), tk(pt_prev)], writes=[("psB", qu)])
                P.op("dve", lambda e, pu=pu, pt_new=pt_new, pt_prev=pt_prev: e.tensor_tensor(
                    out=Wk(ss, pt_new), in0=pu, in1=Wk(ss, pt_prev), op=ALU.add),
                    reads=[("psB", qu), tk(pt_prev)], writes=[tk(pt_new)])
                m_prev, mt_prev, pt_prev = m_new, mt_new, pt_new
                yield
            ptf = pt_prev
            yield
            q6, pu_ = qslot()
            P.op("pe", lambda e: e.matmul(pu_, lhsT=Wk(ss, ptf), rhs=Wk(ss, W_VB), start=True, stop=True),
                 reads=[tk(ptf), tk(W_VB)], writes=[("psB", q6)])
            P.op("act", lambda e: e.activation(out=Wk(ss, W_U), in_=pu_, func=AF.Copy), reads=[("psB", q6)], writes=[tk(W_U)])
            yield
            q7, pw_ = qslot()
            P.op("pe", lambda e: e.matmul(pw_, lhsT=Wk(ss, W_KBG), rhs=Wk(ss, ptf), start=True, stop=True),
                 reads=[tk(W_KBG), tk(ptf)], writes=[("psB", q7)])
            P.op("act", lambda e: e.activation(out=Wk(ss, W_WT), in_=pw_, func=AF.Copy), reads=[("psB", q7)], writes=[tk(W_WT)])
            yield
            while scan_done[h] < blk * 4 + t:
                yield
            q8, p1 = qslot()
            P.op("pe", lambda e: e.matmul(p1, lhsT=Wk(ss, W_WT), rhs=S[:, h, :], start=True, stop=True),
                 reads=[tk(W_WT), ("S", h)], writes=[("psB", q8)])
            P.op("dve", lambda e: e.tensor_tensor(out=Wk(ss, W_VN), in0=Wk(ss, W_U), in1=p1, op=ALU.subtract),
                 reads=[tk(W_U), ("psB", q8)], writes=[tk(W_VN)])
            q9, p2 = qslot()
            P.op("pe", lambda e: e.matmul(p2, lhsT=Wk(ss, W_QG), rhs=S[:, h, :], start=True, stop=False),
                 reads=[tk(W_QG), ("S", h), tk(W_VN), tk(W_INT)], writes=[("psB", q9)])
            P.op("pe", lambda e: e.matmul(p2, lhsT=Wk(ss, W_INT), rhs=Wk(ss, W_VN), start=False, stop=True),
                 reads=[tk(W_INT), tk(W_VN)], writes=[("psB", q9)])
            q10, p3 = qslot()
            P.op("pe", lambda e: e.matmul(p3, lhsT=Wk(ss, W_KD), rhs=Wk(ss, W_VN), start=True, stop=True),
                 reads=[tk(W_KD), tk(W_VN)], writes=[("psB", q10)])
            P.op("dve", lambda e: e.scalar_tensor_tensor(out=S[:, h, :], in0=S[:, h, :], scalar=sm[:, ss, 1:2], in1=p3,
                                                         op0=ALU.mult, op1=ALU.add),
                 reads=[("S", h), ("sm", ss, 1), ("psB", q10)], writes=[("S", h)])
            P.op("act", lambda e: e.activation(out=obuf[:, t, h * 128:(h + 1) * 128], in_=p2, func=AF.Copy),
                 reads=[("psB", q9)], writes=[("obuf", t, h)])
            scan_done[h] += 1

        for _ in prep_gen(0):
            pass
        for blk in range(NB):
            run_chains(blk, prep_gen(blk + 1) if blk + 1 < NB else None)
            epilogue(blk)
        P.emit()


def host_consts():
    import ml_dtypes
    kk = np.arange(128)[:, None, None]
    rr = np.arange(4)[None, :, None]
    qq = np.arange(512)[None, None, :]
    j = np.arange(128)[:, None]
    s_ = np.arange(128)[None, :]
    return {
        "c_ident": np.eye(128, dtype=np.float32),
        "c_identb": np.eye(128, dtype=np.float32).astype(ml_dtypes.bfloat16),
        "c_maskA": (kk + 128 * rr <= qq).astype(np.float32),
        "c_maskB": (kk + 128 * rr < qq).astype(np.float32),
        "c_negU": -(j >= s_).astype(np.float32),
        "c_negO": -np.ones((128, 128), np.float32),
        "c_onesM": np.ones((128, 128), np.float32),
        "c_triU": (j <= s_).astype(np.float32),
        "c_maskLI": (j >= s_).astype(np.float32),
        "c_negLS": -(j > s_).astype(np.float32),
    }


def declare_consts(nc):
    hc = host_consts()
    return {k[2:]: nc.dram_tensor(k, list(v.shape), F32 if v.dtype == np.float32 else BF16, kind="ExternalInput").ap()
            for k, v in hc.items()}


TL = SEQ // 2
WCORE = 3 * NH * 64 * 2 + (4 * NH * 128 + 2 * NH)
WNAMES = {
    "ffn1_w_gu": [DEPTH, D, 2 * DFF], "ffn1_w_down": [DEPTH, DFF, D],
    "ffn2_w_gu": [DEPTH, D, 2 * DFF], "ffn2_w_down": [DEPTH, DFF, D],
    "ln_g": [DEPTH, 3, D], "ln_b": [DEPTH, 3, D], "w_in": [DEPTH, D, WCORE],
    "conv_w": [DEPTH, 128, 3 * NH, 4], "dn_a_log": [DEPTH, NH], "dn_dt_bias": [DEPTH, NH],
    "dn_norm_g": [DEPTH, 128], "diff_lambda": [DEPTH, 128], "diff_norm_g": [DEPTH, 64],
    "sb_norm_g": [DEPTH, 64], "w_out": [DEPTH, 4 * NH * 64, D],
}


def build_program():
    nc = bass.Bass("TRN2", target_bir_lowering=False)
    x = nc.dram_tensor("x", [TL, D], F32, kind="ExternalInput").ap()
    w = {k: nc.dram_tensor(k, shp, F32, kind="ExternalInput").ap() for k, shp in WNAMES.items()}
    cst = declare_consts(nc)
    y = nc.dram_tensor("y", [TL, D], F32, kind="ExternalOutput").ap()
    X1 = nc.dram_tensor("s_x1", [TL, D], F32).ap()
    X2 = nc.dram_tensor("s_x2", [TL, D], F32).ap()
    XL = nc.dram_tensor("s_xl", [TL, D], F32).ap()
    O = nc.dram_tensor("s_o", [SEQ, 4 * NH * 64], F32).ap()
    HA = NH * 64
    cur = x
    for l in range(DEPTH):
        lam_init = 0.8 - 0.6 * math.exp(-0.3 * l)
        agin = [nc.dram_tensor(f"s_agin{l}_{c}", [512, D], BF16).ap() for c in range(4)]
        agout = [nc.dram_tensor(f"s_agout{l}_{c}", [1024, D], BF16).ap() for c in range(4)]
        rsin = [nc.dram_tensor(f"s_rsin{l}_{c}", [1024, D], F32).ap() for c in range(4)]
        rsout = [nc.dram_tensor(f"s_rsout{l}_{c}", [512, D], F32).ap() for c in range(4)]

        def xsrc(g, agout=agout):
            r, j = g // 16, g % 16
            row = r * 512 + (j % 4) * 128
            return agout[j // 4][row:row + 128, :]

        ffn_phase(nc, TL, cur, X1, w["ffn1_w_gu"][l], w["ffn1_w_down"][l], w["ln_g"][l, 0], w["ln_b"][l, 0],
                  cst["ident"], ag=(agin, agout))
        attn_phase(nc, SEQ, "A", xsrc, O, 0, w["w_in"][l][:, 0:3 * HA], w["diff_norm_g"][l], cst,
                   dl_d=w["diff_lambda"][l], lambda_init=lam_init)
        attn_phase(nc, SEQ, "B", xsrc, O, HA, w["w_in"][l][:, 3 * HA:6 * HA], w["sb_norm_g"][l], cst)
        gdn_phase(nc, SEQ, xsrc, O, 2 * HA, w["w_in"][l][:, 6 * HA:WCORE], w["conv_w"][l], w["dn_a_log"][l],
                  w["dn_dt_bias"][l], w["dn_norm_g"][l], cst)
        mixln_phase(nc, O, w["w_out"][l], rsin, rsout, X1, X2, w["ln_g"][l, 1], w["ln_b"][l, 1], cst)
        dst = y if l == DEPTH - 1 else XL
        ffn_phase(nc, TL, X2, dst, w["ffn2_w_gu"][l], w["ffn2_w_down"][l], w["ln_g"][l, 2], w["ln_b"][l, 2], cst["ident"])
        cur = XL
    return nc


def core_weights(r, w_in, conv_w, dn_a_log, dn_dt_bias, w_out):
    hs = [NH * r + i for i in range(NH)]
    cols = []
    for base in (0, 256, 512, 768, 1024, 1280):
        for h in hs:
            cols += list(range(base + h * 64, base + (h + 1) * 64))
    cch = []
    for base in (0, 512, 1024):
        for h in hs:
            cch += list(range(base + h * 128, base + (h + 1) * 128))
    cols += [1536 + c for c in cch]
    for h in hs:
        cols += list(range(3072 + h * 128, 3072 + (h + 1) * 128))
    cols += [3584 + h for h in hs] + [3588 + h for h in hs]
    rows = []
    for h in hs:
        rows += list(range(h * 64, (h + 1) * 64))
    for h in hs:
        rows += list(range(256 + h * 64, 256 + (h + 1) * 64))
    for h in hs:
        rows += list(range(512 + h * 128, 512 + (h + 1) * 128))
    cw = conv_w[:, :, cch]
    cw = cw.reshape(DEPTH, 4, 3 * NH, 128).transpose(0, 3, 2, 1)
    f = lambda a: np.ascontiguousarray(a, dtype=np.float32)
    return {"w_in": f(w_in[:, :, cols]), "conv_w": f(cw), "dn_a_log": f(dn_a_log[:, hs]),
            "dn_dt_bias": f(dn_dt_bias[:, hs]), "w_out": f(w_out[:, rows, :])}


def kernel(x, ffn1_w_gu, ffn1_w_down, ffn2_w_gu, ffn2_w_down, ln_g, ln_b, w_in, conv_w, dn_a_log, dn_dt_bias,
           dn_norm_g, diff_lambda, diff_norm_g, sb_norm_g, w_out):
    f = lambda a: np.ascontiguousarray(np.asarray(a, dtype=np.float32))
    x = f(x)
    B = x.shape[0]
    shared = {
        "ffn1_w_gu": f(ffn1_w_gu), "ffn1_w_down": f(ffn1_w_down), "ffn2_w_gu": f(ffn2_w_gu), "ffn2_w_down": f(ffn2_w_down),
        "ln_g": f(ln_g), "ln_b": f(ln_b), "dn_norm_g": f(dn_norm_g),
        "diff_lambda": f(np.asarray(diff_lambda).reshape(DEPTH, 128)), "diff_norm_g": f(diff_norm_g), "sb_norm_g": f(sb_norm_g),
    }
    shared.update(host_consts())
    per_rank = [core_weights(r, f(w_in), f(conv_w), f(dn_a_log), f(dn_dt_bias), f(w_out)) for r in range(2)]
    nc = build_program()
    in_maps = []
    for c in range(2 * B):
        b, r = c // 2, c % 2
        in_maps.append(dict(shared, **per_rank[r], x=np.ascontiguousarray(x[b, r * TL:(r + 1) * TL])))
    res = run_bass_kernel_spmd(nc, in_maps, core_ids=list(range(2 * B)))
    out = np.empty((B, SEQ, D), np.float32)
    for c in range(2 * B):
        out[c // 2, (c % 2) * TL:(c % 2 + 1) * TL] = np.asarray(res.results[c]["y"], dtype=np.float32)
    return out
```

```python
from contextlib import ExitStack
import math
import numpy as np
import concourse.bass as bass
import concourse.mybir as mybir
from concourse.bass_utils import run_bass_kernel_spmd

F32 = mybir.dt.float32
BF16 = mybir.dt.bfloat16
AF = mybir.ActivationFunctionType
ALU = mybir.AluOpType

D = 1024
DFF = 2816
NFC = DFF // 128
DEPTH = 2
SEQ = 4096
NIN = 3592
ALPHA = (2.0 * DEPTH) ** 0.25
LN_EPS = 1e-5
RMS_EPS = 1e-6

ENGS = ("pe", "act", "dve", "pool", "sp")
RG = [[0, 1], [2, 3], [4, 5], [6, 7]]
NH = 2
BLK = {"pe": "tensor", "act": "scalar", "dve": "vector", "pool": "gpsimd", "sp": "sync"}
SAME_ENG_SYNC = True
_uid = [0]
_tagc = [0]


def _tag():
    _tagc[0] += 1
    return f"_p{_tagc[0]}"


class Op:
    __slots__ = ("eng", "fn", "deps", "signals", "sigval", "isdma", "dsem", "dval", "dinc")


class Prog:
    def __init__(self, nc):
        self.nc = nc
        self.q = {e: [] for e in ENGS}
        self.lastw = {}
        self.readers = {}
        self.dtot = {}

    def op(self, eng, fn, reads=(), writes=(), dsem=None, dinc=16):
        o = Op()
        o.eng, o.fn, o.signals, o.sigval = eng, fn, False, 0
        o.isdma = dsem is not None
        o.dsem = dsem
        o.dval = 0
        o.dinc = dinc
        if o.isdma:
            self.dtot[dsem] = self.dtot.get(dsem, 0) + dinc
            o.dval = self.dtot[dsem]
        deps = []
        for k in reads:
            p = self.lastw.get(k)
            if p is not None:
                deps.append((p, "raw"))
        for k in writes:
            p = self.lastw.get(k)
            if p is not None:
                deps.append((p, "waw"))
            for p in self.readers.get(k, {}).values():
                deps.append((p, "war"))
        o.deps = []
        for p, kind in deps:
            if p is o:
                continue
            if not p.isdma and not o.isdma and p.eng == eng:
                if eng == "pe" or not SAME_ENG_SYNC:
                    continue
            if not p.isdma:
                p.signals = True
            o.deps.append(p)
        for k in reads:
            sk = ("d", dsem) if o.isdma else eng
            self.readers.setdefault(k, {})[sk] = o
        for k in writes:
            self.lastw[k] = o
            self.readers[k] = {}
        self.q[eng].append(o)
        return o

    def emit(self):
        nc = self.nc
        _uid[0] += 1
        u = _uid[0]
        esem = {}
        for e in ENGS:
            c = 0
            for o in self.q[e]:
                if o.signals and not o.isdma:
                    c += 1
                    o.sigval = c
            if c:
                esem[e] = nc.alloc_semaphore(name=f"s{u}_{e}")
        dsems = {}
        for i, k in enumerate(self.dtot):
            dsems[k] = nc.alloc_semaphore(name=f"d{u}_{i}")
        with nc.Block() as block:
            for e in ENGS:
                if not self.q[e] and e != "sp":
                    continue

                def body(eng, e=e):
                    seen = {}
                    for o in self.q[e]:
                        need = {}
                        for p in o.deps:
                            if p.isdma:
                                sk, val = ("d", p.dsem), p.dval
                            else:
                                sk, val = p.eng, p.sigval
                            if val > need.get(sk, 0):
                                need[sk] = val
                        for sk, val in need.items():
                            if seen.get(sk, 0) >= val:
                                continue
                            seen[sk] = val
                            h = dsems[sk[1]] if isinstance(sk, tuple) else esem[sk]
                            eng.wait_ge(h, val)
                        ins = o.fn(eng)
                        if o.isdma:
                            ins.then_inc(dsems[o.dsem], o.dinc)
                        elif o.signals:
                            ins.then_inc(esem[e], 1)
                    if e == "sp":
                        for k, tot in self.dtot.items():
                            eng.wait_ge(dsems[k], tot)

                getattr(block, BLK[e])(body)
        nc.clear_and_free_semaphores(list(esem.values()) + list(dsems.values()))
        nc.all_engine_barrier()


class Rot:
    def __init__(self, n):
        self.n, self.i = n, -1

    def next(self):
        self.i = (self.i + 1) % self.n
        return self.i


def load_cast_weight(P, pool, name, dram_ap, nk, ncols, stage, stage_rot, dst, chunk):
    for k in range(nk):
        for c0 in range(0, ncols, chunk):
            cw = min(chunk, ncols - c0)
            s = stage_rot.next()
            P.op("sp", lambda e, s=s, k=k, c0=c0, cw=cw: e.dma_start(
                out=stage[:, s, 0:cw], in_=dram_ap[k * 128:(k + 1) * 128, c0:c0 + cw]),
                writes=[("stg", s)], dsem=("stg", s))
            P.op(pool, lambda e, s=s, k=k, c0=c0, cw=cw: e.tensor_copy(
                out=dst[:, k, c0:c0 + cw], in_=stage[:, s, 0:cw]),
                reads=[("stg", s)], writes=[(name, k)])


def transpose_block(P, ident, src, src_key, nt, psT, psT_rot, dst, dst_key, evac_engs, nkc=8,
                    scale=None):
    for kc in range(nkc):
        b = psT_rot.next()
        for t in range(nt):
            P.op("pe", lambda e, b=b, t=t, kc=kc: e.transpose(
                out=psT[:, b, t * 128:(t + 1) * 128], in_=src[:, t, kc * 128:(kc + 1) * 128],
                identity=ident[:]),
                reads=[src_key], writes=[("psT", b)])
        ev = evac_engs[kc % len(evac_engs)]
        if ev == "act":
            P.op("act", lambda e, b=b, kc=kc: e.activation(
                out=dst[:, kc, 0:nt * 128], in_=psT[:, b, 0:nt * 128], func=AF.Copy,
                scale=(1.0 if scale is None else scale)),
                reads=[("psT", b)], writes=[(dst_key, kc)])
        else:
            P.op(ev, lambda e, b=b, kc=kc: e.tensor_copy(
                out=dst[:, kc, 0:nt * 128], in_=psT[:, b, 0:nt * 128]),
                reads=[("psT", b)], writes=[(dst_key, kc)])


def ln_tile(P, z, zkey, s, t, gam, bet, eps, st6, mv, rs, cm05, gk="gam", bk="bet"):
    ln_stats(P, z, zkey, s, t, st6, mv)
    P.op("dve", lambda e: e.tensor_scalar(rs[:, t:t + 1], mv[:, t, 1:2], eps, None, ALU.add),
         reads=[("mv", t)], writes=[("rs", t)])
    P.op("pool", lambda e: e.tensor_tensor(out=rs[:, t:t + 1], in0=rs[:, t:t + 1], in1=cm05[:], op=ALU.pow),
         reads=[("rs", t), "cm05"], writes=[("rs", t)])
    P.op("dve", lambda e: e.scalar_tensor_tensor(out=z[:, s, :], in0=z[:, s, :], scalar=mv[:, t, 0:1], in1=gam[:],
                                                 op0=ALU.subtract, op1=ALU.mult),
         reads=[zkey, ("mv", t), gk], writes=[zkey])
    P.op("dve", lambda e: e.scalar_tensor_tensor(out=z[:, s, :], in0=z[:, s, :], scalar=rs[:, t:t + 1], in1=bet[:],
                                                 op0=ALU.mult, op1=ALU.add),
         reads=[zkey, ("rs", t), bk], writes=[zkey])


def ln_stats(P, z, zkey, s, t, st6, mv):
    for h in range(2):
        P.op("dve", lambda e, h=h: e.bn_stats(out=st6[:, t, h, :], in_=z[:, s, h * 512:(h + 1) * 512]),
             reads=[zkey], writes=[("st6", t, h)])
    P.op("dve", lambda e: e.bn_aggr(out=mv[:, t, 0:2], in_=st6[:, t, :, :]),
         reads=[("st6", t, 0), ("st6", t, 1)], writes=[("mv", t)])


def ln_rstd(P, mv, rs, nt, eps):
    P.op("act", lambda e: e.activation(out=rs[:, 0:nt], in_=mv[:, 0:nt, 1], func=AF.Sqrt, bias=eps, scale=1.0),
         reads=[("mv", t) for t in range(nt)], writes=["rs0"])
    P.op("dve", lambda e: e.reciprocal(out=rs[:, 0:nt], in_=rs[:, 0:nt]), reads=["rs0"], writes=["rs"])


def ln_apply(P, z, zkey, s, t, gam, bet, mv, rs, gk="gam", bk="bet"):
    P.op("dve", lambda e: e.tensor_scalar(z[:, s, :], z[:, s, :], mv[:, t, 0:1], rs[:, t:t + 1],
                                          ALU.subtract, ALU.mult),
         reads=[zkey, ("mv", t), "rs"], writes=[zkey])
    P.op("pool", lambda e: e.tensor_tensor(out=z[:, s, :], in0=z[:, s, :], in1=gam[:], op=ALU.mult),
         reads=[zkey, gk], writes=[zkey])
    P.op("pool", lambda e: e.tensor_tensor(out=z[:, s, :], in0=z[:, s, :], in1=bet[:], op=ALU.add),
         reads=[zkey, bk], writes=[zkey])


def ffn_phase(nc, T, x_in, x_out, w_gu, w_down, g_ff, b_ff, ident_d, ag=None):
    NB = T // 512
    NS = 5
    with ExitStack() as es:
        tg = _tag()
        sb = lambda n, s, d: es.enter_context(nc.sbuf_tensor(n + tg, s, d))
        ps = lambda n, s, d: es.enter_context(nc.psum_tensor(n + tg, s, d))
        wgu = sb("wgu", [128, 8, 2 * DFF], BF16)
        wd = sb("wd", [128, NFC, D], BF16)
        stage = sb("stage", [128, 4, 512], F32)
        xs = sb("xs", [128, NS, D], F32)
        xT = sb("xT", [128, 8, 512], BF16)
        aT = sb("aT", [128, NFC, 512], BF16)
        sg = sb("sg", [128, 2, 512], F32)
        gam = sb("gam", [128, D], F32)
        bet = sb("bet", [128, D], F32)
        ident = sb("ident", [128, 128], F32)
        st6 = sb("st6", [128, 4, 2, 6], F32)
        mv = sb("mv", [128, 4, 2], F32)
        rs = sb("rs", [128, 4], F32)
        xb = sb("xb", [128, D], BF16)
        cm05 = sb("cm05", [128, 1], F32)
        psT = ps("psT", [128, 2, 512], F32)
        psG = ps("psG", [128, 2, 512], F32)
        psU = ps("psU", [128, 2, 512], F32)
        psY = ps("psY", [128, 2, 512], F32)

        P = Prog(nc)
        P.op("sp", lambda e: e.dma_start(out=ident[:], in_=ident_d[:, :]), writes=["ident"], dsem="c0")
        P.op("sp", lambda e: e.dma_start(out=gam[:], in_=g_ff.partition_broadcast(128)), writes=["gam"], dsem="c1")
        P.op("sp", lambda e: e.dma_start(out=bet[:], in_=b_ff.partition_broadcast(128)), writes=["bet"], dsem="c2")

        NT = T // 128

        def load_tile(g):
            if g >= NT:
                return
            s = g % NS
            P.op("sp", lambda e: e.dma_start(out=xs[:, s, :], in_=x_in[g * 128:(g + 1) * 128, :]),
                 writes=[("xs", s)], dsem=("xl", s))

        for g in range(NS):
            load_tile(g)
        srot = Rot(4)
        ceng = ["pool", "dve", "act", "dve"]
        cnt = [0]

        def wload(dst, dkey, src, k, c0, cw):
            sidx = srot.next()
            P.op("sp", lambda e: e.dma_start(out=stage[:, sidx, 0:cw], in_=src[k * 128:(k + 1) * 128, c0:c0 + cw]),
                 writes=[("stg", sidx)], dsem=("stg", sidx))
            en = ceng[cnt[0] % len(ceng)]
            cnt[0] += 1
            if en == "act":
                P.op("act", lambda e: e.activation(out=dst[:, k, c0:c0 + cw], in_=stage[:, sidx, 0:cw], func=AF.Copy),
                     reads=[("stg", sidx)], writes=[dkey])
            else:
                P.op(en, lambda e: e.tensor_copy(out=dst[:, k, c0:c0 + cw], in_=stage[:, sidx, 0:cw]),
                     reads=[("stg", sidx)], writes=[dkey])

        def load_gu_group(cgi):
            cg = cgi * 512
            cw_ = min(512, DFF - cg)
            for half in range(2):
                for kc in range(8):
                    wload(wgu, ("wgu", kc, half, cgi), w_gu, kc, half * DFF + cg, cw_)

        def load_wd():
            for fc in range(NFC):
                for h in range(2):
                    wload(wd, ("wd", fc, h), w_down, fc, h * 512, 512)

        P.op("pool", lambda e: e.memset(cm05[:], -0.5), writes=["cm05"])
        rT, rG, rY = Rot(2), Rot(2), Rot(2)
        for blk in range(NB):
            sl = [(blk * 4 + t) % NS for t in range(4)]
            for kc in range(8):
                b = rT.next()
                for t in range(4):
                    P.op("pe", lambda e, b=b, t=t, kc=kc, s=sl[t]: e.transpose(
                        out=psT[:, b, t * 128:(t + 1) * 128], in_=xs[:, s, kc * 128:(kc + 1) * 128],
                        identity=ident[:]),
                        reads=[("xs", sl[t]), "ident"], writes=[("psT", b)])
                P.op("act", lambda e, b=b, kc=kc: e.activation(out=xT[:, kc, :], in_=psT[:, b, :], func=AF.Copy),
                     reads=[("psT", b)], writes=[("xT", kc)])
            for fc in range(NFC):
                if blk == 0 and fc % 4 == 0:
                    load_gu_group(fc // 4)
                b = rG.next()
                for kc in range(8):
                    P.op("pe", lambda e, b=b, kc=kc, fc=fc: e.matmul(
                        psG[:, b, :], lhsT=wgu[:, kc, fc * 128:(fc + 1) * 128], rhs=xT[:, kc, :],
                        start=(kc == 0), stop=(kc == 7)),
                        reads=[("wgu", kc, 0, fc // 4), ("xT", kc)], writes=[("psG", b)])
                for kc in range(8):
                    P.op("pe", lambda e, b=b, kc=kc, fc=fc: e.matmul(
                        psU[:, b, :], lhsT=wgu[:, kc, DFF + fc * 128:DFF + (fc + 1) * 128], rhs=xT[:, kc, :],
                        start=(kc == 0), stop=(kc == 7)),
                        reads=[("wgu", kc, 1, fc // 4), ("xT", kc)], writes=[("psU", b)])
                P.op("act", lambda e, b=b: e.activation(out=sg[:, b, :], in_=psG[:, b, :], func=AF.Silu),
                     reads=[("psG", b)], writes=[("sg", b)])
                P.op("dve", lambda e, b=b, fc=fc: e.tensor_tensor(
                    out=aT[:, fc, :], in0=psU[:, b, :], in1=sg[:, b, :], op=ALU.mult),
                    reads=[("psU", b), ("sg", b)], writes=[("aT", fc)])
            if blk == 0:
                load_wd()
            for t in range(4):
                s = sl[t]
                for h in range(2):
                    b = rY.next()
                    for fc in range(NFC):
                        P.op("pe", lambda e, b=b, fc=fc, t=t, h=h: e.matmul(
                            psY[:, b, :], lhsT=aT[:, fc, t * 128:(t + 1) * 128], rhs=wd[:, fc, h * 512:(h + 1) * 512],
                            start=(fc == 0), stop=(fc == NFC - 1)),
                            reads=[("aT", fc), ("wd", fc, h)], writes=[("psY", b)])
                    P.op("dve", lambda e, b=b, h=h, s=s: e.scalar_tensor_tensor(
                        out=xs[:, s, h * 512:(h + 1) * 512], in0=psY[:, b, :], scalar=0.5 / ALPHA,
                        in1=xs[:, s, h * 512:(h + 1) * 512], op0=ALU.mult, op1=ALU.add),
                        reads=[("psY", b), ("xs", s)], writes=[("xs", s)])
                ln_tile(P, xs, ("xs", s), s, t, gam, bet, LN_EPS / ALPHA ** 2, st6, mv, rs, cm05)
                g = blk * 4 + t
                P.op("pool", lambda e, s=s, g=g: e.dma_start(out=x_out[g * 128:(g + 1) * 128, :], in_=xs[:, s, :]),
                     reads=[("xs", s)], dsem=("xst", s))
                if ag is not None:
                    P.op("act", lambda e, s=s: e.activation(out=xb[:], in_=xs[:, s, :], func=AF.Copy),
                         reads=[("xs", s)], writes=["xb"])
                    P.op("sp", lambda e, blk=blk, t=t: e.dma_start(out=ag[0][blk][t * 128:(t + 1) * 128, :], in_=xb[:]),
                         reads=["xb"], writes=[("agin", blk, t)], dsem="xbst")
                load_tile(g + NS)
            if ag is not None:
                P.op("pool", lambda e, blk=blk: e.collective_compute(
                    "AllGather", ALU.bypass, replica_groups=RG, ins=[ag[0][blk].opt()], outs=[ag[1][blk].opt()]),
                    reads=[("agin", blk, t) for t in range(4)], writes=[("agout", blk)], dsem="cc", dinc=1)
        P.emit()


def mixln_phase(nc, o_in, w_o, rsin, rsout, x_in, x_out, g_d, b_d, cst):
    with ExitStack() as es:
        tg = _tag()
        sb = lambda n, s, d: es.enter_context(nc.sbuf_tensor(n + tg, s, d))
        ps = lambda n, s, d: es.enter_context(nc.psum_tensor(n + tg, s, d))
        wo = sb("wo", [128, 4, D], BF16)
        stage = sb("stage", [128, 2, D], F32)
        os_ = sb("os", [128, 3, 512], F32)
        oT = sb("oT", [128, 2, 4, 128], BF16)
        ys = sb("ys", [128, 2, D], F32)
        ident = sb("ident", [128, 128], F32)
        xs = sb("xs", [128, 3, D], F32)
        ms = sb("ms", [128, 3, D], F32)
        gam = sb("gam", [128, D], F32)
        bet = sb("bet", [128, D], F32)
        st6 = sb("st6", [128, 2, 2, 6], F32)
        mv = sb("mv", [128, 2, 2], F32)
        rs = sb("rs", [128, 2], F32)
        cm05 = sb("cm05", [128, 1], F32)
        psT = ps("psT", [128, 2, 512], F32)
        psY = ps("psY", [128, 2, 512], F32)
        P = Prog(nc)
        P.op("pool", lambda e: e.memset(cm05[:], -0.5), writes=["cm05"])
        P.op("sp", lambda e: e.dma_start(out=ident[:], in_=cst["ident"][:, :]), writes=["ident"], dsem="c0")
        P.op("sp", lambda e: e.dma_start(out=gam[:], in_=g_d.partition_broadcast(128)), writes=["gam"], dsem="c1")
        P.op("sp", lambda e: e.dma_start(out=bet[:], in_=b_d.partition_broadcast(128)), writes=["bet"], dsem="c2")
        order = [(c, rr, j) for c in range(4) for rr in range(2) for j in range(4)]

        def load_tile(i):
            if i >= len(order):
                return
            c, rr, j = order[i]
            g = rr * 16 + c * 4 + j
            s = i % 3
            P.op("sp", lambda e: e.dma_start(out=os_[:, s, :], in_=o_in[g * 128:(g + 1) * 128, :]),
                 writes=[("os", s)], dsem=("ol", s))

        for i in range(3):
            load_tile(i)
        load_cast_weight(P, "dve", "wo", w_o, 4, D, stage, Rot(2), wo, D)
        rT, rY = Rot(2), Rot(2)

        def tile_body(i):
            c, rr, j = order[i]
            s, so = i % 3, i % 2
            b = rT.next()
            for kc in range(4):
                P.op("pe", lambda e, kc=kc: e.transpose(
                    out=psT[:, b, kc * 128:(kc + 1) * 128], in_=os_[:, s, kc * 128:(kc + 1) * 128], identity=ident[:]),
                    reads=[("os", s), "ident"], writes=[("psT", b)])
            P.op("act", lambda e: e.activation(out=oT[:, so, :, :], in_=psT[:, b, :].rearrange("p (j t) -> p j t", j=4), func=AF.Copy),
                 reads=[("psT", b)], writes=[("oT", so)])
            load_tile(i + 3)
            for h in range(2):
                by = rY.next()
                for kc in range(4):
                    P.op("pe", lambda e, kc=kc, h=h, by=by: e.matmul(
                        psY[:, by, :], lhsT=oT[:, so, kc, :], rhs=wo[:, kc, h * 512:(h + 1) * 512], start=(kc == 0), stop=(kc == 3)),
                        reads=[("oT", so), ("wo", kc)], writes=[("psY", by)])
                P.op("act", lambda e, by=by, h=h: e.activation(out=ys[:, so, h * 512:(h + 1) * 512], in_=psY[:, by, :], func=AF.Copy),
                     reads=[("psY", by)], writes=[("ys", so, h)])
            row = rr * 512 + j * 128
            P.op("sp", lambda e: e.dma_start(out=rsin[c][row:row + 128, :], in_=ys[:, so, :]),
                 reads=[("ys", so, 0), ("ys", so, 1)], writes=[("rsin", c, rr, j)], dsem=("yst", so))
            if rr == 1 and j == 3:
                P.op("pool", lambda e: e.collective_compute(
                    "ReduceScatter", ALU.add, replica_groups=RG, ins=[rsin[c].opt()], outs=[rsout[c].opt()]),
                    reads=[("rsin", c, a, b_) for a in range(2) for b_ in range(4)], writes=[("rsout", c)], dsem="cc", dinc=1)

        def ln_load(g):
            s = g % 3
            P.op("pool", lambda e: e.dma_start(out=xs[:, s, :], in_=x_in[g * 128:(g + 1) * 128, :]),
                 writes=[("xs", s)], dsem=("xl", s))
            P.op("pool", lambda e: e.dma_start(out=ms[:, s, :], in_=rsout[g // 4][(g % 4) * 128:(g % 4 + 1) * 128, :]),
                 reads=[("rsout", g // 4)], writes=[("ms", s)], dsem=("ml", s))

        def ln_tile_body(g):
            s = g % 3
            P.op("dve", lambda e: e.scalar_tensor_tensor(out=xs[:, s, :], in0=ms[:, s, :], scalar=1.0 / ALPHA, in1=xs[:, s, :],
                                                         op0=ALU.mult, op1=ALU.add),
                 reads=[("ms", s), ("xs", s)], writes=[("xs", s)])
            ln_tile(P, xs, ("xs", s), s, g % 2, gam, bet, LN_EPS / ALPHA ** 2, st6, mv, rs, cm05)
            P.op("pool", lambda e: e.dma_start(out=x_out[g * 128:(g + 1) * 128, :], in_=xs[:, s, :]),
                 reads=[("xs", s)], dsem=("xst", s))

        def ln_chunk(c):
            for g in range(4 * c, 4 * c + 3):
                ln_load(g)
            for g in range(4 * c, 4 * c + 4):
                ln_tile_body(g)
                if g + 3 < 4 * c + 4:
                    ln_load(g + 3)

        for c in range(4):
            for i in range(8 * c, 8 * c + 8):
                tile_body(i)
            if c >= 1:
                ln_chunk(c - 1)
        ln_chunk(3)
        P.emit()


def attn_phase(nc, T, kind, xsrc, o_out, ocol0, w_q, gn_d, cst, dl_d=None, lambda_init=0.0, dbg=None):
    HW = NH * 64
    NT, NB = T // 128, T // 512
    isA = kind == "A"
    dvw = 65 if isA else 64
    qscale = 32 ** -0.5 if isA else 0.125
    with ExitStack() as es:
        tg = _tag()
        sb = lambda n, s, d: es.enter_context(nc.sbuf_tensor(n + tg, s, d))
        ps = lambda n, s, d: es.enter_context(nc.psum_tensor(n + tg, s, d))
        wq = sb("wq", [128, 8, 3 * HW], BF16)
        stage = sb("stage", [128, 2, 3 * HW], F32)
        xs = sb("xs", [128, 5, D], BF16)
        xT = sb("xT", [128, 8, 512], BF16)
        qT = sb("qT", [64, NH, T], BF16)
        kT = sb("kT", [64, NH, T], BF16)
        vv = sb("vv", [128, NT, NH, dvw], BF16)
        ident = sb("ident", [128, 128], BF16)
        msk = sb("msk", [128, 4, 512], F32)
        pT = sb("pT", [128, 3, 512], BF16)
        oacc = sb("oacc", [128, NT, NH, 64], F32)
        ssq = sb("ssq", [128, NT * NH], F32)
        junk = sb("junk", [128, 64], F32)
        gn = sb("gn", [128, 64], F32)
        rec = sb("rec", [128, 2, 4], F32)
        if isA:
            dl = sb("dl", [128, 4, 32], F32)
            dlp = sb("dlp", [128, 2, 32], F32)
            dls = sb("dls", [128, 2], F32)
            negl = sb("negl", [128, 1], F32)
        else:
            e32 = sb("e32", [128, 2, 512], F32)
            lp = sb("lp", [128, 3, 512], F32)
            lpacc = sb("lpacc", [128, 2, 512], F32)
            negU = sb("negU", [128, 128], F32)
            negO = sb("negO", [128, 128], F32)
        psX = ps("psX", [128, 1, 1024], BF16)
        psA = ps("psA", [128, 7, 512], F32)

        P = Prog(nc)
        P.op("sp", lambda e: e.dma_start(out=ident[:], in_=cst["identb"][:, :]), writes=["ident"], dsem="c0")
        P.op("sp", lambda e: e.dma_start(out=msk[:], in_=cst["maskA" if isA else "maskB"][:, :, :]), writes=["msk"], dsem="c1")
        P.op("sp", lambda e: e.dma_start(out=gn[:], in_=gn_d.partition_broadcast(128)), writes=["gn"], dsem="c2")
        if isA:
            P.op("sp", lambda e: e.dma_start(out=dl[:], in_=dl_d.partition_broadcast(128)), writes=["dl"], dsem="c3")
            P.op("dve", lambda e: e.tensor_tensor(out=dlp[:], in0=dl[:, 0:4:2, :], in1=dl[:, 1:4:2, :], op=ALU.mult),
                 reads=["dl"], writes=["dlp"])
            P.op("dve", lambda e: e.reduce_sum(out=dls[:], in_=dlp[:], axis=mybir.AxisListType.X),
                 reads=["dlp"], writes=["dls"])
            P.op("act", lambda e: e.activation(out=dls[:], in_=dls[:], func=AF.Exp), reads=["dls"], writes=["dls"])
            P.op("dve", lambda e: e.tensor_tensor(out=negl[:], in0=dls[:, 1:2], in1=dls[:, 0:1], op=ALU.subtract),
                 reads=["dls"], writes=["negl"])
            P.op("dve", lambda e: e.tensor_scalar(negl[:], negl[:], -lambda_init, None, ALU.add),
                 reads=["negl"], writes=["negl"])
            P.op("pool", lambda e: e.memset(vv[:, :, :, 64:65], 1.0), writes=["vones"])
        else:
            P.op("sp", lambda e: e.dma_start(out=negU[:], in_=cst["negU"][:, :]), writes=["negU"], dsem="c4")
            P.op("sp", lambda e: e.dma_start(out=negO[:], in_=cst["negO"][:, :]), writes=["negO"], dsem="c5")

        def load_tile(g):
            if g >= NT:
                return
            s = g % 5
            P.op("sp", lambda e: e.dma_start(out=xs[:, s, :], in_=xsrc(g)),
                 writes=[("xs", s)], dsem=("xl", s))

        for g in range(5):
            load_tile(g)
        load_cast_weight(P, "pool", "wq", w_q, 8, 3 * HW, stage, Rot(2), wq, 3 * HW)

        rT = Rot(2)
        for blk in range(NB):
            sl = [(blk * 4 + t) % 5 for t in range(4)]
            for kc in range(8):
                for t in range(4):
                    P.op("pe", lambda e, t=t, kc=kc, s=sl[t]: e.transpose(
                        out=psX[:, 0, t * 128:(t + 1) * 128], in_=xs[:, s, kc * 128:(kc + 1) * 128], identity=ident[:]),
                        reads=[("xs", sl[t]), "ident"], writes=["psX"])
                P.op("act" if kc % 2 else "dve", lambda e, kc=kc: e.tensor_copy(out=xT[:, kc, :], in_=psX[:, 0, 0:512]) if kc % 2 == 0 else
                     e.activation(out=xT[:, kc, :], in_=psX[:, 0, 0:512], func=AF.Copy),
                     reads=["psX"], writes=[("xT", kc)])
            for t in range(4):
                load_tile(blk * 4 + t + 5)
            for which, dst, sc, nm in ((0, qT, qscale, "qT"), (HW, kT, 1.0, "kT")):
                for h in range(NH):
                    b = rT.next()
                    for kc in range(8):
                        P.op("pe", lambda e, b=b, kc=kc, c=which + h * 64: e.matmul(
                            psA[0:64, b, :], lhsT=wq[:, kc, c:c + 64], rhs=xT[:, kc, :], start=(kc == 0), stop=(kc == 7)),
                            reads=[("wq", kc), ("xT", kc)], writes=[("psA", b)])
                    P.op("act", lambda e, b=b, h=h, dst=dst, sc=sc, blk=blk: e.activation(
                        out=dst[:, h, blk * 512:(blk + 1) * 512], in_=psA[0:64, b, :], func=AF.Copy, scale=sc),
                        reads=[("psA", b)], writes=[(nm, h, blk)])
            for t in range(4):
                b = rT.next()
                g = blk * 4 + t
                for kc in range(8):
                    P.op("pe", lambda e, b=b, kc=kc, t=t: e.matmul(
                        psA[:, b, 0:HW], lhsT=xT[:, kc, t * 128:(t + 1) * 128], rhs=wq[:, kc, 2 * HW:3 * HW],
                        start=(kc == 0), stop=(kc == 7)),
                        reads=[("wq", kc), ("xT", kc)], writes=[("psA", b)])
                P.op("dve", lambda e, b=b, g=g: e.tensor_copy(
                    out=vv[:, g, :, 0:64], in_=psA[:, b, 0:HW].rearrange("p (h d) -> p h d", h=NH)),
                    reads=[("psA", b)], writes=[("vv", g)])

        SBK = [0, 1, 2, 3, 4] if isA else [0, 1]
        WBK = [2, 3, 4]
        OBK = [5, 6]
        rS, rW, rO, rP, rE, rL = Rot(len(SBK)), Rot(3), Rot(2), Rot(3), Rot(2), Rot(3)
        steps = []
        for J in range(NB):
            for h in range(NH):
                for i in range(2 if isA else 1):
                    ks = list(range(0, 4 * J + 4)) if isA else list(range(4 * J + 3, -1, -1))
                    grp = {}
                    for n_, kb in enumerate(ks):
                        steps.append(dict(J=J, h=h, i=i, kb=kb, r=kb - 4 * J, first=(n_ == 0), last=(n_ == len(ks) - 1), grp=grp))
        la_state = [0]

        def stage_scores(st):
            J, h, i, kb, r = st["J"], st["h"], st["i"], st["kb"], st["r"]
            if st["first"]:
                st["grp"]["bo"] = OBK[rO.next()]
                st["grp"]["pvfirst"] = True
            bs = SBK[rS.next()]
            st["bs"] = bs
            if isA:
                lk, rq = kT[32 * i:32 * i + 32, h, kb * 128:(kb + 1) * 128], qT[32 * i:32 * i + 32, h, J * 512:(J + 1) * 512]
            else:
                lk, rq = kT[:, h, kb * 128:(kb + 1) * 128], qT[:, h, J * 512:(J + 1) * 512]
            st["lk"], st["rq"] = lk, rq
            P.op("pe", lambda e: e.matmul(psA[:, bs, :], lhsT=lk, rhs=rq, start=True, stop=True),
                 reads=[("kT", h, kb // 4), ("qT", h, J)], writes=[("psA", bs)])
            if isA:
                sp_ = rP.next()
                st["sp"] = sp_
                P.op("act", lambda e: e.activation(out=pT[:, sp_, :], in_=psA[:, bs, :], func=AF.Exp),
                     reads=[("psA", bs)], writes=[("pT", sp_)])
                if r >= 0:
                    P.op("pool", lambda e: e.tensor_tensor(out=pT[:, sp_, :], in0=pT[:, sp_, :], in1=msk[:, r, :], op=ALU.mult),
                         reads=[("pT", sp_), "msk"], writes=[("pT", sp_)])
            else:
                se, sl_ = rE.next(), rL.next()
                st["sl"] = sl_
                bw = WBK[rW.next()]
                st["bw"] = bw
                P.op("pe", lambda e: e.matmul(psA[:, bw, :], lhsT=lk, rhs=rq, start=True, stop=False),
                     reads=[("kT", h, kb // 4), ("qT", h, J)], writes=[("psA", bw)])
                P.op("act", lambda e: e.activation(out=e32[:, se, :], in_=psA[:, bs, :], func=AF.Exp),
                     reads=[("psA", bs)], writes=[("e32", se)])
                P.op("act", lambda e: e.activation(out=lp[:, sl_, :], in_=e32[:, se, :], func=AF.Ln, bias=1.0),
                     reads=[("e32", se)], writes=[("lp", sl_)])
                if r >= 0:
                    P.op("pool", lambda e: e.tensor_tensor(out=lp[:, sl_, :], in0=lp[:, sl_, :], in1=msk[:, r, :], op=ALU.mult),
                         reads=[("lp", sl_), "msk"], writes=[("lp", sl_)])

        def stage_cumsum(st):
            r, bw, sl_ = st["r"], st["bw"], st["sl"]
            first, last = st["first"], st["last"]
            la = la_state[0]
            P.op("pe", lambda e: e.matmul(psA[:, bw, :], lhsT=negU[:], rhs=lp[:, sl_, :], start=False, stop=first),
                 reads=["negU", ("lp", sl_)], writes=[("psA", bw)])
            if not first:
                P.op("pe", lambda e: e.matmul(psA[:, bw, :], lhsT=negO[:], rhs=lpacc[:, la, :], start=False, stop=True),
                     reads=["negO", ("lpacc", la)], writes=[("psA", bw)])
            sp_ = rP.next()
            st["sp"] = sp_
            P.op("act", lambda e: e.activation(out=pT[:, sp_, :], in_=psA[:, bw, :], func=AF.Exp),
                 reads=[("psA", bw)], writes=[("pT", sp_)])
            if r >= 0:
                P.op("pool", lambda e: e.tensor_tensor(out=pT[:, sp_, :], in0=pT[:, sp_, :], in1=msk[:, r, :], op=ALU.mult),
                     reads=[("pT", sp_), "msk"], writes=[("pT", sp_)])
            if not last:
                if first:
                    P.op("dve", lambda e: e.tensor_copy(out=lpacc[:, 1 - la, :], in_=lp[:, sl_, :]),
                         reads=[("lp", sl_)], writes=[("lpacc", 1 - la)])
                else:
                    P.op("dve", lambda e: e.tensor_tensor(out=lpacc[:, 1 - la, :], in0=lpacc[:, la, :], in1=lp[:, sl_, :], op=ALU.add),
                         reads=[("lpacc", la), ("lp", sl_)], writes=[("lpacc", 1 - la)])
                la_state[0] = 1 - la

        def stage_pv(st):
            J, h, i, kb, r, sp_ = st["J"], st["h"], st["i"], st["kb"], st["r"], st["sp"]
            grp = st["grp"]
            bo = grp["bo"]
            for c in range(max(r, 0), 4):
                first = grp["pvfirst"]
                grp["pvfirst"] = False
                P.op("pe", lambda e, c=c, first=first: e.matmul(
                    psA[:, bo, c * dvw:(c + 1) * dvw], lhsT=pT[:, sp_, c * 128:(c + 1) * 128], rhs=vv[:, kb, h, 0:dvw],
                    start=first, stop=False, skip_group_check=True),
                    reads=[("pT", sp_), ("vv", kb)] + (["vones"] if isA else []), writes=[("psA", bo)])
            if not st["last"]:
                return
            if isA:
                P.op("dve", lambda e: e.reciprocal(out=rec[:, i, :], in_=psA[:, bo, 64:260:65]),
                     reads=[("psA", bo)], writes=[("rec", i)])
                if i == 1:
                    P.op("dve", lambda e: e.tensor_scalar(rec[:, 1, :], rec[:, 1, :], negl[:, 0:1], None, ALU.mult),
                         reads=[("rec", 1), "negl"], writes=[("rec", 1)])
                for c in range(4):
                    g = J * 4 + c
                    if i == 0:
                        P.op("dve", lambda e, c=c, g=g: e.tensor_scalar(
                            oacc[:, g, h, :], psA[:, bo, c * 65:c * 65 + 64], rec[:, 0, c:c + 1], None, ALU.mult),
                            reads=[("psA", bo), ("rec", 0)], writes=[("oacc", g, h)])
                    else:
                        P.op("dve", lambda e, c=c, g=g: e.scalar_tensor_tensor(
                            out=oacc[:, g, h, :], in0=psA[:, bo, c * 65:c * 65 + 64], scalar=rec[:, 1, c:c + 1],
                            in1=oacc[:, g, h, :], op0=ALU.mult, op1=ALU.add),
                            reads=[("psA", bo), ("rec", 1), ("oacc", g, h)], writes=[("oacc", g, h)])
            else:
                for c in range(4):
                    g = J * 4 + c
                    P.op("dve", lambda e, c=c, g=g: e.tensor_copy(out=oacc[:, g, h, :], in_=psA[:, bo, c * 64:(c + 1) * 64]),
                         reads=[("psA", bo)], writes=[("oacc", g, h)])
            if (not isA) or i == 1:
                for c in range(4):
                    g = J * 4 + c
                    P.op("act", lambda e, g=g: e.activation(
                        out=junk[:], in_=oacc[:, g, h, :], func=AF.Square, accum_out=ssq[:, g * NH + h:g * NH + h + 1]),
                        reads=[("oacc", g, h)], writes=["junk", ("ssq", g, h)])

        N = len(steps)
        if isA:
            for n in range(N + 1):
                if n < N:
                    stage_scores(steps[n])
                if n >= 1:
                    stage_pv(steps[n - 1])
        else:
            for n in range(N + 2):
                if n < N:
                    stage_scores(steps[n])
                if 1 <= n <= N:
                    stage_cumsum(steps[n - 1])
                if n >= 2:
                    stage_pv(steps[n - 2])
        allssq = [("ssq", g, h) for g in range(NT) for h in range(NH)]
        P.op("act", lambda e: e.activation(out=ssq[:], in_=ssq[:], func=AF.Sqrt, bias=RMS_EPS, scale=1.0 / 64),
             reads=allssq, writes=["rs0"])
        P.op("dve", lambda e: e.reciprocal(out=ssq[:], in_=ssq[:]), reads=["rs0"], writes=["rs1"])
        if isA:
            P.op("dve", lambda e: e.tensor_scalar(ssq[:], ssq[:], 1.0 - lambda_init, None, ALU.mult),
                 reads=["rs1"], writes=["rs1"])
        for g in range(NT):
            for h in range(NH):
                P.op("dve", lambda e, g=g, h=h: e.scalar_tensor_tensor(
                    out=oacc[:, g, h, :], in0=oacc[:, g, h, :], scalar=ssq[:, g * NH + h:g * NH + h + 1],
                    in1=gn[:], op0=ALU.mult, op1=ALU.mult),
                    reads=[("oacc", g, h), "rs1", "gn"], writes=[("oacc", g, h)])
            P.op("sp", lambda e, g=g: e.dma_start(
                out=o_out[g * 128:(g + 1) * 128, ocol0:ocol0 + HW], in_=oacc[:, g, :, :].rearrange("p h d -> p (h d)")),
                reads=[("oacc", g, h) for h in range(NH)], dsem=("ost", g % 4))
        P.emit()


W_GB, W_DM, W_DMI, W_NDMS, W_N, W_NT, W_IN, W_INT, W_EG, W_QG = range(10)
W_MA, W_MB, W_MTA, W_MTB, W_PTA, W_PTB, W_KBG, W_KD, W_VB, W_U, W_WT, W_VN, W_GR = range(10, 23)
NTMP = 23


def gdn_phase(nc, T, xsrc, o_out, ocol0, w_c, cw_d, alog_d, dtb_d, gn_d, cst):
    NT, NB = T // 128, T // 512
    NS = 4
    NCH = 3 * NH
    ZO = NCH * 128
    BO = ZO + NH * 128
    WC = BO + 2 * NH
    HD = NH * 128
    with ExitStack() as es:
        tg = _tag()
        sb = lambda n, s, d: es.enter_context(nc.sbuf_tensor(n + tg, s, d))
        ps = lambda n, s, d: es.enter_context(nc.psum_tensor(n + tg, s, d))
        wC = sb("wC", [128, 8, WC], BF16)
        stage = sb("stage", [128, 2, WC], F32)
        xs = sb("xs", [128, NS, D], BF16)
        identb = sb("identb", [128, 128], BF16)
        sq2 = sb("sq2", [128, 4, HD], F32)
        xT = sb("xT", [128, 8, 512], BF16)
        pc = sb("pc", [128, NCH, 515], F32)
        csA = sb("cs", [128, 2, NCH, 512], F32)
        sq = sb("sq", [128, 4, 512], F32)
        ktA = sb("kt", [128, 2, 4, NH, 128], F32)
        vtA = sb("vt", [128, 2, 4, NH, 128], F32)
        zsA = sb("zs", [128, 2, 4, HD], F32)
        ba = sb("ba", [128, 4, 2 * NH], F32)
        betaA = sb("beta", [128, 2, 4, NH], F32)
        ggA = sb("gg", [128, 2, 4, NH], F32)
        dtb = sb("dtb", [128, 4, NH], F32)
        nega = sb("nega", [128, 4, NH], F32)
        cw = sb("cw", [128, NCH, 4], F32)
        ident = sb("ident", [128, 128], F32)
        onesM = sb("onesM", [128, 128], F32)
        triU = sb("triU", [128, 128], F32)
        maskLI = sb("maskLI", [128, 128], F32)
        negLS = sb("negLS", [128, 128], F32)
        epsb = sb("epsb", [128, 1], F32)
        gnC = sb("gnC", [128, 128], F32)
        S = sb("S", [128, NH, 128], F32)
        gcum = sb("gcum", [128, 4, NH], F32)
        egc = sb("egc", [128, 4, NH], F32)
        bg4 = sb("bg4", [128, 4, NH], F32)
        sm = sb("sm", [128, 6, 4], F32)
        Wt = sb("Wt", [128, 6, NTMP, 128], F32)
        obuf = sb("obuf", [128, 4, HD], F32)
        ssq = sb("ssq", [128, 4 * NH], F32)
        psX = ps("psX", [128, 2, 1024], BF16)
        psB = ps("psB", [128, 6, 512], F32)

        P = Prog(nc)
        ci = [0]

        def cload(dst, src, key):
            ci[0] += 1
            P.op("sp", lambda e: e.dma_start(out=dst, in_=src), writes=[key], dsem=("c", ci[0]))

        cload(ident[:], cst["ident"][:, :], "ident")
        cload(identb[:], cst["identb"][:, :], "identb")
        cload(onesM[:], cst["onesM"][:, :], "onesM")
        cload(triU[:], cst["triU"][:, :], "triU")
        cload(maskLI[:], cst["maskLI"][:, :], "maskLI")
        cload(negLS[:], cst["negLS"][:, :], "negLS")
        cload(cw[:], cw_d[:, :, :], "cw")
        cload(gnC[:], gn_d.partition_broadcast(128), "gnC")
        for t in range(4):
            cload(dtb[:, t, :], dtb_d.partition_broadcast(128), ("dtb", t))
            cload(nega[:, t, :], alog_d.partition_broadcast(128), ("nega", t))
        P.op("act", lambda e: e.activation(out=nega[:], in_=nega[:], func=AF.Exp),
             reads=[("nega", t) for t in range(4)], writes=["nega"])
        P.op("dve", lambda e: e.tensor_scalar(nega[:], nega[:], -1.0, None, ALU.mult), reads=["nega"], writes=["nega"])
        P.op("pool", lambda e: e.memset(epsb[:], RMS_EPS), writes=["epsb"])
        P.op("pool", lambda e: e.memset(S[:], 0.0), writes=[("S", h) for h in range(NH)])
        P.op("pool", lambda e: e.memset(pc[:, :, 0:3], 0.0), writes=[("pc", ch) for ch in range(NCH)])

        def load_tile(g):
            if g >= NT:
                return
            s = g % NS
            P.op("sp", lambda e: e.dma_start(out=xs[:, s, :], in_=xsrc(g)),
                 writes=[("xs", s)], dsem=("xl", s))

        for g in range(NS):
            load_tile(g)
        load_cast_weight(P, "pool", "wC", w_c, 8, WC, stage, Rot(2), wC, WC)

        rT, rN, rE, rX = Rot(6), Rot(4), Rot(2), Rot(2)
        Wk = lambda ss, i: Wt[:, ss, i, :]

        def qslot():
            q = rT.next()
            return q, psB[:, q, 0:128]

        def prep_gen(blk):
            pb = blk % 2
            cs, kt, vt, zs, beta, gg = csA[:, pb], ktA[:, pb], vtA[:, pb], zsA[:, pb], betaA[:, pb], ggA[:, pb]
            sl = [(blk * 4 + t) % NS for t in range(4)]
            for kc in range(8):
                bx = rX.next()
                for t in range(4):
                    P.op("pe", lambda e, t=t, kc=kc, s=sl[t], bx=bx: e.transpose(
                        out=psX[:, bx, t * 128:(t + 1) * 128], in_=xs[:, s, kc * 128:(kc + 1) * 128], identity=identb[:]),
                        reads=[("xs", sl[t]), "identb"], writes=[("psX", bx)])
                P.op("dve", lambda e, kc=kc, bx=bx: e.tensor_copy(out=xT[:, kc, :], in_=psX[:, bx, 0:512]),
                     reads=[("psX", bx)], writes=[("xT", kc)])
                if kc % 2:
                    yield
            for t in range(4):
                load_tile(blk * 4 + t + NS)
            for ch in range(NCH):
                b = rT.next()
                for kc in range(8):
                    P.op("pe", lambda e, b=b, kc=kc, ch=ch: e.matmul(
                        psB[:, b, :], lhsT=wC[:, kc, ch * 128:(ch + 1) * 128], rhs=xT[:, kc, :], start=(kc == 0), stop=(kc == 7)),
                        reads=[("wC", kc), ("xT", kc)], writes=[("psB", b)])
                P.op("act", lambda e, b=b, ch=ch: e.activation(out=pc[:, ch, 3:515], in_=psB[:, b, :], func=AF.Copy),
                     reads=[("psB", b)], writes=[("pc", ch)])
                yield
            for t in range(4):
                b = rT.next()
                for kc in range(8):
                    P.op("pe", lambda e, b=b, kc=kc, t=t: e.matmul(
                        psB[:, b, 0:HD], lhsT=xT[:, kc, t * 128:(t + 1) * 128], rhs=wC[:, kc, ZO:BO], start=(kc == 0), stop=(kc == 7)),
                        reads=[("wC", kc), ("xT", kc)], writes=[("psB", b)])
                P.op("act", lambda e, b=b, t=t: e.activation(out=zs[:, t, :], in_=psB[:, b, 0:HD], func=AF.Silu),
                     reads=[("psB", b)], writes=[("zs", pb, t)])
                yield
            for t in range(4):
                q, qa = qslot()
                for kc in range(8):
                    P.op("pe", lambda e, qa=qa, kc=kc, t=t: e.matmul(
                        qa[:, 0:2 * NH], lhsT=xT[:, kc, t * 128:(t + 1) * 128], rhs=wC[:, kc, BO:WC], start=(kc == 0), stop=(kc == 7)),
                        reads=[("wC", kc), ("xT", kc)], writes=[("psB", q)])
                P.op("dve", lambda e, qa=qa, t=t: e.tensor_copy(out=ba[:, t, :], in_=qa[:, 0:2 * NH]),
                     reads=[("psB", q)], writes=[("ba", t)])
                yield
            for ch in range(NCH):
                P.op("dve", lambda e, ch=ch: e.tensor_scalar(cs[:, ch, :], pc[:, ch, 0:512], cw[:, ch, 0:1], None, ALU.mult),
                     reads=[("pc", ch), "cw"], writes=[("cs", pb, ch)])
                for i in range(1, 4):
                    P.op("dve", lambda e, ch=ch, i=i: e.scalar_tensor_tensor(
                        out=cs[:, ch, :], in0=pc[:, ch, i:i + 512], scalar=cw[:, ch, i:i + 1], in1=cs[:, ch, :],
                        op0=ALU.mult, op1=ALU.add),
                        reads=[("pc", ch), "cw", ("cs", pb, ch)], writes=[("cs", pb, ch)])
                P.op("act", lambda e, ch=ch: e.activation(out=cs[:, ch, :], in_=cs[:, ch, :], func=AF.Silu),
                     reads=[("cs", pb, ch)], writes=[("cs", pb, ch)])
                yield
            P.op("pool", lambda e: e.tensor_copy(out=pc[:, :, 0:3], in_=pc[:, :, 512:515]),
                 reads=[("pc", ch) for ch in range(NCH)], writes=[("pc", ch) for ch in range(NCH)])
            bak = [("ba", t) for t in range(4)]
            P.op("act", lambda e: e.activation(out=beta[:], in_=ba[:, :, 0:NH], func=AF.Exp, scale=-1.0), reads=bak, writes=[("beta", pb)])
            P.op("dve", lambda e: e.tensor_scalar(beta[:], beta[:], 1.0, None, ALU.add), reads=[("beta", pb)], writes=[("beta", pb)])
            P.op("dve", lambda e: e.reciprocal(out=beta[:], in_=beta[:]), reads=[("beta", pb)], writes=[("beta", pb)])
            P.op("dve", lambda e: e.tensor_tensor(out=gg[:], in0=ba[:, :, NH:2 * NH], in1=dtb[:], op=ALU.add),
                 reads=bak + [("dtb", t) for t in range(4)], writes=[("gg", pb)])
            P.op("act", lambda e: e.activation(out=gg[:], in_=gg[:], func=AF.Exp), reads=[("gg", pb)], writes=[("gg", pb)])
            P.op("act", lambda e: e.activation(out=gg[:], in_=gg[:], func=AF.Ln, bias=1.0), reads=[("gg", pb)], writes=[("gg", pb)])
            P.op("dve", lambda e: e.tensor_tensor(out=gg[:], in0=gg[:], in1=nega[:], op=ALU.mult), reads=[("gg", pb), "nega"], writes=[("gg", pb)])
            yield
            nsl = [rN.next() for _ in range(2 * NH)]
            nbk = []
            for ch in range(2 * NH):
                s2 = nsl[ch]
                P.op("act", lambda e, ch=ch, s2=s2: e.activation(out=sq[:, s2, :], in_=cs[:, ch, :], func=AF.Square),
                     reads=[("cs", pb, ch)], writes=[("sq", s2)])
            yield
            for ch in range(2 * NH):
                s2 = nsl[ch]
                bn = rT.next()
                nbk.append(bn)
                P.op("pe", lambda e, s2=s2, bn=bn: e.matmul(psB[:, bn, :], lhsT=onesM[:], rhs=sq[:, s2, :], start=True, stop=True),
                     reads=["onesM", ("sq", s2)], writes=[("psB", bn)])
                P.op("act", lambda e, s2=s2, bn=bn: e.activation(out=sq[:, s2, :], in_=psB[:, bn, :], func=AF.Ln, bias=epsb[:, 0:1]),
                     reads=[("psB", bn), "epsb"], writes=[("sq", s2)])
                yield
            for ch in range(2 * NH):
                s2 = nsl[ch]
                P.op("act", lambda e, s2=s2: e.activation(out=sq[:, s2, :], in_=sq[:, s2, :], func=AF.Exp, scale=-0.5),
                     reads=[("sq", s2)], writes=[("sq", s2)])
                if ch < NH:
                    P.op("dve", lambda e, ch=ch, s2=s2: e.scalar_tensor_tensor(
                        out=cs[:, ch, :], in0=cs[:, ch, :], scalar=128 ** -0.5, in1=sq[:, s2, :], op0=ALU.mult, op1=ALU.mult),
                        reads=[("cs", pb, ch), ("sq", s2)], writes=[("cs", pb, ch)])
                else:
                    P.op("dve", lambda e, ch=ch, s2=s2: e.tensor_tensor(out=cs[:, ch, :], in0=cs[:, ch, :], in1=sq[:, s2, :], op=ALU.mult),
                         reads=[("cs", pb, ch), ("sq", s2)], writes=[("cs", pb, ch)])
            yield
            for dst, nm, c0 in ((kt, "kt", NH), (vt, "vt", 2 * NH)):
                for hh in range(NH):
                    b = rT.next()
                    for t in range(4):
                        P.op("pe", lambda e, b=b, t=t, ch=c0 + hh: e.transpose(
                            out=psB[:, b, t * 128:(t + 1) * 128], in_=cs[:, ch, t * 128:(t + 1) * 128], identity=ident[:]),
                            reads=[("cs", pb, c0 + hh), "ident"], writes=[("psB", b)])
                    P.op("act", lambda e, b=b, hh=hh, dst=dst: e.activation(
                        out=dst[:, :, hh, :], in_=psB[:, b, :].rearrange("p (t d) -> p t d", t=4), func=AF.Copy),
                        reads=[("psB", b)], writes=[(nm, pb, hh)])
                    yield

        def run_chains(blk, extra):
            chains = [(t, h) for t in range(4) for h in range(NH)]
            active, idx = ([extra] if extra is not None else []), 0
            while idx < len(chains) or active:
                while len(active) < (7 if extra is not None else 6) and idx < len(chains):
                    t, h = chains[idx]
                    idx += 1
                    if h == 0:
                        chunk_body(blk, t)
                    active.append(head_body(blk, t, h, t))
                for gen in list(active):
                    try:
                        next(gen)
                    except StopIteration:
                        active.remove(gen)

        def epilogue(blk):
            pb = blk % 2
            zs = zsA[:, pb]
            P.op("act", lambda e: e.activation(out=sq2[:], in_=obuf[:], func=AF.Square),
                 reads=[("obuf", t, h) for t in range(4) for h in range(NH)], writes=["sq2"])
            P.op("dve", lambda e: e.reduce_sum(out=ssq[:], in_=sq2[:].rearrange("p t (h d) -> p (t h) d", h=NH),
                                               axis=mybir.AxisListType.X),
                 reads=["sq2"], writes=["ssq"])
            P.op("act", lambda e: e.activation(out=ssq[:], in_=ssq[:], func=AF.Ln, scale=1.0 / 128, bias=epsb[:, 0:1]),
                 reads=["ssq", "epsb"], writes=["ssq"])
            P.op("act", lambda e: e.activation(out=ssq[:], in_=ssq[:], func=AF.Exp, scale=-0.5), reads=["ssq"], writes=["ssq"])
            for t in range(4):
                for h in range(NH):
                    P.op("dve", lambda e, t=t, h=h: e.scalar_tensor_tensor(
                        out=obuf[:, t, h * 128:(h + 1) * 128], in0=obuf[:, t, h * 128:(h + 1) * 128],
                        scalar=ssq[:, t * NH + h:t * NH + h + 1], in1=gnC[:], op0=ALU.mult, op1=ALU.mult),
                        reads=[("obuf", t, h), "ssq", "gnC"], writes=[("obuf", t, h)])
                P.op("pool", lambda e, t=t: e.tensor_tensor(out=obuf[:, t, :], in0=obuf[:, t, :], in1=zs[:, t, :], op=ALU.mult),
                     reads=[("obuf", t, h) for h in range(NH)] + [("zs", pb, t)], writes=[("obuf", t, h) for h in range(NH)])
                g = blk * 4 + t
                P.op("sp", lambda e, t=t, g=g: e.dma_start(out=o_out[g * 128:(g + 1) * 128, ocol0:ocol0 + HD], in_=obuf[:, t, :]),
                     reads=[("obuf", t, h) for h in range(NH)], dsem=("ost", t))

        scan_done = [0] * NH

        def chunk_body(blk, t):
            pb = blk % 2
            beta, gg = betaA[:, pb], ggA[:, pb]
            cp = t
            q, qa = qslot()
            P.op("pe", lambda e: e.matmul(qa[:, 0:NH], lhsT=triU[:], rhs=gg[:, t, :], start=True, stop=True),
                 reads=["triU", ("gg", pb)], writes=[("psB", q)])
            P.op("dve", lambda e: e.tensor_copy(out=gcum[:, cp, :], in_=qa[:, 0:NH]), reads=[("psB", q)], writes=[("gcum", cp)])
            P.op("act", lambda e: e.activation(out=egc[:, cp, :], in_=gcum[:, cp, :], func=AF.Exp),
                 reads=[("gcum", cp)], writes=[("egc", cp)])
            P.op("dve", lambda e: e.tensor_tensor(out=bg4[:, cp, :], in0=beta[:, t, :], in1=egc[:, cp, :], op=ALU.mult),
                 reads=[("beta", pb), ("egc", cp)], writes=[("bg4", cp)])

        def head_body(blk, t, h, cp):
            pb = blk % 2
            cs, kt, vt, beta, gg = csA[:, pb], ktA[:, pb], vtA[:, pb], betaA[:, pb], ggA[:, pb]
            ss = (t * NH + h) % 6
            tk = lambda i: ("W", ss, i)
            tsl = slice(t * 128, (t + 1) * 128)
            qTa, kTa = cs[:, h, tsl], cs[:, NH + h, tsl]
            bcol = beta[:, t, h:h + 1]
            gcol = gcum[:, cp, h:h + 1]
            P.op("act", lambda e: e.activation(out=Wk(ss, W_GB), in_=onesM[:], func=AF.Copy, scale=gg[:, t, h:h + 1]),
                 reads=["onesM", ("gg", pb)], writes=[tk(W_GB)])
            q1, grow = qslot()
            P.op("pe", lambda e: e.matmul(grow, lhsT=Wk(ss, W_GB), rhs=triU[:], start=True, stop=True),
                 reads=[tk(W_GB), "triU"], writes=[("psB", q1)])
            P.op("dve", lambda e: e.tensor_copy(out=Wk(ss, W_GR), in_=grow), reads=[("psB", q1)], writes=[tk(W_GR)])
            P.op("act", lambda e: e.activation(out=Wk(ss, W_EG), in_=Wk(ss, W_GR), func=AF.Exp), reads=[tk(W_GR)], writes=[tk(W_EG)])
            P.op("act", lambda e: e.activation(out=sm[:, ss, 1:2], in_=Wt[:, ss, W_GR, 127:128], func=AF.Exp),
                 reads=[tk(W_GR)], writes=[("sm", ss, 1)])
            yield
            P.op("dve", lambda e: e.tensor_scalar(Wk(ss, W_DM), Wk(ss, W_GR), gcol, 0.0, ALU.subtract, ALU.max),
                 reads=[tk(W_GR), ("gcum", cp)], writes=[tk(W_DM)])
            P.op("act", lambda e: e.activation(out=Wk(ss, W_DM), in_=Wk(ss, W_DM), func=AF.Exp, scale=-1.0),
                 reads=[tk(W_DM)], writes=[tk(W_DM)])
            P.op("dve", lambda e: e.tensor_tensor(out=Wk(ss, W_DMI), in0=Wk(ss, W_DM), in1=maskLI[:], op=ALU.mult),
                 reads=[tk(W_DM), "maskLI"], writes=[tk(W_DMI)])
            P.op("pool", lambda e: e.tensor_tensor(out=Wk(ss, W_NDMS), in0=Wk(ss, W_DM), in1=negLS[:], op=ALU.mult),
                 reads=[tk(W_DM), "negLS"], writes=[tk(W_NDMS)])
            yield
            P.op("dve", lambda e: e.tensor_tensor(out=sm[:, ss, 2:3], in0=Wt[:, ss, W_GR, 127:128], in1=gcol, op=ALU.subtract),
                 reads=[tk(W_GR), ("gcum", cp)], writes=[("sm", ss, 2)])
            P.op("act", lambda e: e.activation(out=sm[:, ss, 3:4], in_=sm[:, ss, 2:3], func=AF.Exp),
                 reads=[("sm", ss, 2)], writes=[("sm", ss, 3)])
            yield
            P.op("act", lambda e: e.activation(out=Wk(ss, W_KBG), in_=kt[:, t, h, :], func=AF.Copy, scale=bg4[:, cp, h:h + 1]),
                 reads=[("kt", pb, h), ("bg4", cp)], writes=[tk(W_KBG)])
            P.op("act", lambda e: e.activation(out=Wk(ss, W_KD), in_=kt[:, t, h, :], func=AF.Copy, scale=sm[:, ss, 3:4]),
                 reads=[("kt", pb, h), ("sm", ss, 3)], writes=[tk(W_KD)])
            P.op("act", lambda e: e.activation(out=Wk(ss, W_VB), in_=vt[:, t, h, :], func=AF.Copy, scale=bcol),
                 reads=[("vt", pb, h), ("beta", pb)], writes=[tk(W_VB)])
            yield
            q2, gk = qslot()
            P.op("pe", lambda e: e.matmul(gk, lhsT=kTa, rhs=kTa, start=True, stop=True),
                 reads=[("cs", pb, NH + h)], writes=[("psB", q2)])
            P.op("dve", lambda e: e.scalar_tensor_tensor(out=Wk(ss, W_N), in0=gk, scalar=bcol, in1=Wk(ss, W_NDMS),
                                                         op0=ALU.mult, op1=ALU.mult),
                 reads=[("psB", q2), ("beta", pb), tk(W_NDMS)], writes=[tk(W_N)])
            yield
            q3, ntp = qslot()
            P.op("pe", lambda e: e.transpose(out=ntp, in_=Wk(ss, W_N), identity=ident[:]),
                 reads=[tk(W_N), "ident"], writes=[("psB", q3)])
            P.op("act", lambda e: e.activation(out=Wk(ss, W_NT), in_=ntp, func=AF.Copy), reads=[("psB", q3)], writes=[tk(W_NT)])
            yield
            q4, qk = qslot()
            P.op("pe", lambda e: e.matmul(qk, lhsT=qTa, rhs=kTa, start=True, stop=True),
                 reads=[("cs", pb, h), ("cs", pb, NH + h)], writes=[("psB", q4)])
            P.op("dve", lambda e: e.tensor_tensor(out=Wk(ss, W_IN), in0=qk, in1=Wk(ss, W_DMI), op=ALU.mult),
                 reads=[("psB", q4), tk(W_DMI)], writes=[tk(W_IN)])
            yield
            q5, itp = qslot()
            P.op("pe", lambda e: e.transpose(out=itp, in_=Wk(ss, W_IN), identity=ident[:]),
                 reads=[tk(W_IN), "ident"], writes=[("psB", q5)])
            P.op("act", lambda e: e.activation(out=Wk(ss, W_INT), in_=itp, func=AF.Copy), reads=[("psB", q5)], writes=[tk(W_INT)])
            yield
            P.op("pool", lambda e: e.tensor_tensor(out=Wk(ss, W_QG), in0=qTa, in1=Wk(ss, W_EG), op=ALU.mult),
                 reads=[("cs", pb, h), tk(W_EG)], writes=[tk(W_QG)])
            yield
            P.op("pool", lambda e: e.tensor_tensor(out=Wk(ss, W_PTA), in0=Wk(ss, W_NT), in1=ident[:], op=ALU.add),
                 reads=[tk(W_NT), "ident"], writes=[tk(W_PTA)])
            m_prev, mt_prev, pt_prev = W_N, W_NT, W_PTA
            for j in range(1, 7):
                m_new = W_MA if j % 2 else W_MB
                mt_new = W_MTA if j % 2 else W_MTB
                pt_new = W_PTB if j % 2 else W_PTA
                qm, pm = qslot()
                P.op("pe", lambda e, pm=pm, a=mt_prev, b_=m_prev: e.matmul(pm, lhsT=Wk(ss, a), rhs=Wk(ss, b_), start=True, stop=True),
                     reads=[tk(mt_prev), tk(m_prev)], writes=[("psB", qm)])
                P.op("act", lambda e, pm=pm, m_new=m_new: e.activation(out=Wk(ss, m_new), in_=pm, func=AF.Copy),
                     reads=[("psB", qm)], writes=[tk(m_new)])
                if j < 6:
                    qn, pn = qslot()
                    P.op("pe", lambda e, pn=pn, a=m_prev, b_=mt_prev: e.matmul(pn, lhsT=Wk(ss, a), rhs=Wk(ss, b_), start=True, stop=True),
                         reads=[tk(m_prev), tk(mt_prev)], writes=[("psB", qn)])
                    P.op("dve", lambda e, pn=pn, mt_new=mt_new: e.tensor_copy(out=Wk(ss, mt_new), in_=pn),
                         reads=[("psB", qn)], writes=[tk(mt_new)])
                yield
                qu, pu = qslot()
                P.op("pe", lambda e, pu=pu, m_new=m_new, pt_prev=pt_prev: e.matmul(pu, lhsT=Wk(ss, m_new), rhs=Wk(ss, pt_prev), start=True, stop=True),
                     reads=[tk(m_new), tk(pt_prev)], writes=[("psB", qu)])
                P.op("dve", lambda e, pu=pu, pt_new=pt_new, pt_prev=pt_prev: e.tensor_tensor(
                    out=Wk(ss, pt_new), in0=pu, in1=Wk(ss, pt_prev), op=ALU.add),
                    reads=[("psB", qu), tk(pt_prev)], writes=[tk(pt_new)])
                m_prev, mt_prev, pt_prev = m_new, mt_new, pt_new
                yield
            ptf = pt_prev
            yield
            q6, pu_ = qslot()
            P.op("pe", lambda e: e.matmul(pu_, lhsT=Wk(ss, ptf), rhs=Wk(ss, W_VB), start=True, stop=True),
                 reads=[tk(ptf), tk(W_VB)], writes=[("psB", q6)])
            P.op("act", lambda e: e.activation(out=Wk(ss, W_U), in_=pu_, func=AF.Copy), reads=[("psB", q6)], writes=[tk(W_U)])
            yield
            q7, pw_ = qslot()
            P.op("pe", lambda e: e.matmul(pw_, lhsT=Wk(ss, W_KBG), rhs=Wk(ss, ptf), start=True, stop=True),
                 reads=[tk(W_KBG), tk(ptf)], writes=[("psB", q7)])
            P.op("act", lambda e: e.activation(out=Wk(ss, W_WT), in_=pw_, func=AF.Copy), reads=[("psB", q7)], writes=[tk(W_WT)])
            yield
            while scan_done[h] < blk * 4 + t:
                yield
            q8, p1 = qslot()
            P.op("pe", lambda e: e.matmul(p1, lhsT=Wk(ss, W_WT), rhs=S[:, h, :], start=True, stop=True),
                 reads=[tk(W_WT), ("S", h)], writes=[("psB", q8)])
            P.op("dve", lambda e: e.tensor_tensor(out=Wk(ss, W_VN), in0=Wk(ss, W_U), in1=p1, op=ALU.subtract),
                 reads=[tk(W_U), ("psB", q8)], writes=[tk(W_VN)])
            q9, p2 = qslot()
            P.op("pe", lambda e: e.matmul(p2, lhsT=Wk(ss, W_QG), rhs=S[:, h, :], start=True, stop=False),
                 reads=[tk(W_QG), ("S", h), tk(W_VN), tk(W_INT)], writes=[("psB", q9)])
            P.op("pe", lambda e: e.matmul(p2, lhsT=Wk(ss, W_INT), rhs=Wk(ss, W_VN), start=False, stop=True),
                 reads=[tk(W_INT), tk(W_VN)], writes=[("psB", q9)])
            q10, p3 = qslot()
            P.op("pe", lambda e: e.matmul(p3, lhsT=Wk(ss, W_KD), rhs=Wk(ss, W_VN), start=True, stop=True),
                 reads=[tk(W_KD), tk(W_VN)], writes=[("psB", q10)])
            P.op("dve", lambda e: e.scalar_tensor_tensor(out=S[:, h, :], in0=S[:, h, :], scalar=sm[:, ss, 1:2], in1=p3,
                                                         op0=ALU.mult, op1=ALU.add),
                 reads=[("S", h), ("sm", ss, 1), ("psB", q10)], writes=[("S", h)])
            P.op("act", lambda e: e.activation(out=obuf[:, t, h * 128:(h + 1) * 128], in_=p2, func=AF.Copy),
                 reads=[("psB", q9)], writes=[("obuf", t, h)])
            scan_done[h] += 1

        for _ in prep_gen(0):
            pass
        for blk in range(NB):
            run_chains(blk, prep_gen(blk + 1) if blk + 1 < NB else None)
            epilogue(blk)
        P.emit()


def host_consts():
    import ml_dtypes
    kk = np.arange(128)[:, None, None]
    rr = np.arange(4)[None, :, None]
    qq = np.arange(512)[None, None, :]
    j = np.arange(128)[:, None]
    s_ = np.arange(128)[None, :]
    return {
        "c_ident": np.eye(128, dtype=np.float32),
        "c_identb": np.eye(128, dtype=np.float32).astype(ml_dtypes.bfloat16),
        "c_maskA": (kk + 128 * rr <= qq).astype(np.float32),
        "c_maskB": (kk + 128 * rr < qq).astype(np.float32),
        "c_negU": -(j >= s_).astype(np.float32),
        "c_negO": -np.ones((128, 128), np.float32),
        "c_onesM": np.ones((128, 128), np.float32),
        "c_triU": (j <= s_).astype(np.float32),
        "c_maskLI": (j >= s_).astype(np.float32),
        "c_negLS": -(j > s_).astype(np.float32),
    }


def declare_consts(nc):
    hc = host_consts()
    return {k[2:]: nc.dram_tensor(k, list(v.shape), F32 if v.dtype == np.float32 else BF16, kind="ExternalInput").ap()
            for k, v in hc.items()}


TL = SEQ // 2
WCORE = 3 * NH * 64 * 2 + (4 * NH * 128 + 2 * NH)
WNAMES = {
    "ffn1_w_gu": [DEPTH, D, 2 * DFF], "ffn1_w_down": [DEPTH, DFF, D],
    "ffn2_w_gu": [DEPTH, D, 2 * DFF], "ffn2_w_down": [DEPTH, DFF, D],
    "ln_g": [DEPTH, 3, D], "ln_b": [DEPTH, 3, D], "w_in": [DEPTH, D, WCORE],
    "conv_w": [DEPTH, 128, 3 * NH, 4], "dn_a_log": [DEPTH, NH], "dn_dt_bias": [DEPTH, NH],
    "dn_norm_g": [DEPTH, 128], "diff_lambda": [DEPTH, 128], "diff_norm_g": [DEPTH, 64],
    "sb_norm_g": [DEPTH, 64], "w_out": [DEPTH, 4 * NH * 64, D],
}


def build_program():
    nc = bass.Bass("TRN2", target_bir_lowering=False)
    x = nc.dram_tensor("x", [TL, D], F32, kind="ExternalInput").ap()
    w = {k: nc.dram_tensor(k, shp, F32, kind="ExternalInput").ap() for k, shp in WNAMES.items()}
    cst = declare_consts(nc)
    y = nc.dram_tensor("y", [TL, D], F32, kind="ExternalOutput").ap()
    X1 = nc.dram_tensor("s_x1", [TL, D], F32).ap()
    X2 = nc.dram_tensor("s_x2", [TL, D], F32).ap()
    XL = nc.dram_tensor("s_xl", [TL, D], F32).ap()
    O = nc.dram_tensor("s_o", [SEQ, 4 * NH * 64], F32).ap()
    HA = NH * 64
    cur = x
    for l in range(DEPTH):
        lam_init = 0.8 - 0.6 * math.exp(-0.3 * l)
        agin = [nc.dram_tensor(f"s_agin{l}_{c}", [512, D], BF16).ap() for c in range(4)]
        agout = [nc.dram_tensor(f"s_agout{l}_{c}", [1024, D], BF16).ap() for c in range(4)]
        rsin = [nc.dram_tensor(f"s_rsin{l}_{c}", [1024, D], F32).ap() for c in range(4)]
        rsout = [nc.dram_tensor(f"s_rsout{l}_{c}", [512, D], F32).ap() for c in range(4)]

        def xsrc(g, agout=agout):
            r, j = g // 16, g % 16
            row = r * 512 + (j % 4) * 128
            return agout[j // 4][row:row + 128, :]

        ffn_phase(nc, TL, cur, X1, w["ffn1_w_gu"][l], w["ffn1_w_down"][l], w["ln_g"][l, 0], w["ln_b"][l, 0],
                  cst["ident"], ag=(agin, agout))
        attn_phase(nc, SEQ, "A", xsrc, O, 0, w["w_in"][l][:, 0:3 * HA], w["diff_norm_g"][l], cst,
                   dl_d=w["diff_lambda"][l], lambda_init=lam_init)
        attn_phase(nc, SEQ, "B", xsrc, O, HA, w["w_in"][l][:, 3 * HA:6 * HA], w["sb_norm_g"][l], cst)
        gdn_phase(nc, SEQ, xsrc, O, 2 * HA, w["w_in"][l][:, 6 * HA:WCORE], w["conv_w"][l], w["dn_a_log"][l],
                  w["dn_dt_bias"][l], w["dn_norm_g"][l], cst)
        mixln_phase(nc, O, w["w_out"][l], rsin, rsout, X1, X2, w["ln_g"][l, 1], w["ln_b"][l, 1], cst)
        dst = y if l == DEPTH - 1 else XL
        ffn_phase(nc, TL, X2, dst, w["ffn2_w_gu"][l], w["ffn2_w_down"][l], w["ln_g"][l, 2], w["ln_b"][l, 2], cst["ident"])
        cur = XL
    return nc


def core_weights(r, w_in, conv_w, dn_a_log, dn_dt_bias, w_out):
    hs = [NH * r + i for i in range(NH)]
    cols = []
    for base in (0, 256, 512, 768, 1024, 1280):
        for h in hs:
            cols += list(range(base + h * 64, base + (h + 1) * 64))
    cch = []
    for base in (0, 512, 1024):
        for h in hs:
            cch += list(range(base + h * 128, base + (h + 1) * 128))
    cols += [1536 + c for c in cch]
    for h in hs:
        cols += list(range(3072 + h * 128, 3072 + (h + 1) * 128))
    cols += [3584 + h for h in hs] + [3588 + h for h in hs]
    rows = []
    for h in hs:
        rows += list(range(h * 64, (h + 1) * 64))
    for h in hs:
        rows += list(range(256 + h * 64, 256 + (h + 1) * 64))
    for h in hs:
        rows += list(range(512 + h * 128, 512 + (h + 1) * 128))
    cw = conv_w[:, :, cch]
    cw = cw.reshape(DEPTH, 4, 3 * NH, 128).transpose(0, 3, 2, 1)
    f = lambda a: np.ascontiguousarray(a, dtype=np.float32)
    return {"w_in": f(w_in[:, :, cols]), "conv_w": f(cw), "dn_a_log": f(dn_a_log[:, hs]),
            "dn_dt_bias": f(dn_dt_bias[:, hs]), "w_out": f(w_out[:, rows, :])}


def kernel(x, ffn1_w_gu, ffn1_w_down, ffn2_w_gu, ffn2_w_down, ln_g, ln_b, w_in, conv_w, dn_a_log, dn_dt_bias,
           dn_norm_g, diff_lambda, diff_norm_g, sb_norm_g, w_out):
    f = lambda a: np.ascontiguousarray(np.asarray(a, dtype=np.float32))
    x = f(x)
    B = x.shape[0]
    shared = {
        "ffn1_w_gu": f(ffn1_w_gu), "ffn1_w_down": f(ffn1_w_down), "ffn2_w_gu": f(ffn2_w_gu), "ffn2_w_down": f(ffn2_w_down),
        "ln_g": f(ln_g), "ln_b": f(ln_b), "dn_norm_g": f(dn_norm_g),
        "diff_lambda": f(np.asarray(diff_lambda).reshape(DEPTH, 128)), "diff_norm_g": f(diff_norm_g), "sb_norm_g": f(sb_norm_g),
    }
    shared.update(host_consts())
    per_rank = [core_weights(r, f(w_in), f(conv_w), f(dn_a_log), f(dn_dt_bias), f(w_out)) for r in range(2)]
    nc = build_program()
    in_maps = []
    for c in range(2 * B):
        b, r = c // 2, c % 2
        in_maps.append(dict(shared, **per_rank[r], x=np.ascontiguousarray(x[b, r * TL:(r + 1) * TL])))
    res = run_bass_kernel_spmd(nc, in_maps, core_ids=list(range(2 * B)))
    out = np.empty((B, SEQ, D), np.float32)
    for c in range(2 * B):
        out[c // 2, (c % 2) * TL:(c % 2 + 1) * TL] = np.asarray(res.results[c]["y"], dtype=np.float32)
    return out
```
